# Optimizing a Trainium2 kernel written in Bass

```python
import jax, jax.numpy as jnp
from jax import lax
import numpy as np

D_MODEL = 1024
BATCH = 4
SEQ = 4096
DEPTH = 2

D_A = 2 * D_MODEL
CHUNK = 128
G_A = D_A // 128
N_HEADS_B = 8
HEAD_DIM_B = 128
D_B = N_HEADS_B * HEAD_DIM_B
PATTERNS = ((128, 1), (512, 4), (2048, 16))
N_SIDE = 64
BLK = 64
NEG = -1e30
EPS = 1e-6

SPLIT_WIDTHS = (D_A, D_A, D_A, D_B, D_B, D_B, D_B, D_MODEL, D_MODEL)
N_IN = sum(SPLIT_WIDTHS)
SPLIT_OFFS = tuple(int(o) for o in np.cumsum(SPLIT_WIDTHS)[:-1])

kernel_name = "hybrid_gmlp_dilated_attn_encoder"


def rms_norm(x, g):
    x32 = x.astype(jnp.float32)
    y = x32 * lax.rsqrt(jnp.mean(x32 * x32, axis=-1, keepdims=True) + EPS)
    return (y * g.astype(jnp.float32)).astype(x.dtype)


def layer_norm(x, g, b):
    x32 = x.astype(jnp.float32)
    mu = jnp.mean(x32, axis=-1, keepdims=True)
    var = jnp.mean(jnp.square(x32 - mu), axis=-1, keepdims=True)
    y = (x32 - mu) * lax.rsqrt(var + EPS)
    return (y * g.astype(jnp.float32) + b.astype(jnp.float32)).astype(x.dtype)


def alibi_slopes(n):
    return jnp.exp2(-8.0 * (jnp.arange(n, dtype=jnp.float32) + 1.0) / n)


def spatial_gating(u, v, ln_g, ln_b, w_s, b_s):
    B, S, C = v.shape
    v = layer_norm(v, ln_g, ln_b)
    vc = v.reshape(B, S // CHUNK, CHUNK, G_A, C // G_A)
    s = jnp.einsum('gpq,bcqge->bcpge', w_s, vc) + b_s.T[:, :, None]
    return u * s.reshape(B, S, C)


def dilated_window_attention(q, k, v, dilation, slopes):
    B, S, H, E = q.shape
    L = S // dilation
    nb = -(-L // BLK)
    Lp = nb * BLK

    def to_sub(t):
        return t.reshape(B, L, dilation, H, E).transpose(0, 2, 3, 1, 4)

    qs = jnp.pad(to_sub(q), [(0, 0)] * 3 + [(0, Lp - L), (0, 0)]).reshape(B, dilation, H, nb, BLK, E)

    def key_blocks(t):
        tp = jnp.pad(to_sub(t), [(0, 0)] * 3 + [(BLK, Lp - L + BLK), (0, 0)])
        tp = tp.reshape(B, dilation, H, nb + 2, BLK, E)
        return jnp.concatenate([tp[:, :, :, :-2], tp[:, :, :, 1:-1], tp[:, :, :, 2:]], axis=4)

    kb = key_blocks(k)
    vb = key_blocks(v)
    qa = jnp.arange(BLK)
    kc = jnp.arange(3 * BLK)
    rel = kc[None, :] - BLK - qa[:, None]
    j_idx = (jnp.arange(nb)[:, None] - 1) * BLK + kc[None, :]
    valid = (jnp.abs(rel)[None] <= N_SIDE) & ((j_idx >= 0) & (j_idx < L))[:, None, :]
    dist = (dilation * jnp.abs(rel)).astype(jnp.float32)

    s = jnp.einsum('bdhnqe,bdhnke->bdhnqk', qs, kb).astype(jnp.float32) * (E ** -0.5)
    s = s - slopes[:, None, None, None] * dist
    s = jnp.where(valid, s, NEG)
    lse = jax.nn.logsumexp(s, axis=-1)
    p = jnp.exp(s - lse[..., None]).astype(v.dtype)
    o = jnp.einsum('bdhnqk,bdhnke->bdhnqe', p, vb)
    o = o.reshape(B, dilation, H, Lp, E)[:, :, :, :L].transpose(0, 3, 1, 2, 4).reshape(B, S, H, E)
    lse = lse.reshape(B, dilation, H, Lp)[..., :L].transpose(0, 3, 1, 2).reshape(B, S, H)
    return o, lse


def mixture_of_dilations(q, k, v):
    slopes = alibi_slopes(N_HEADS_B)
    outs, lses = [], []
    for _, dilation in PATTERNS:
        o, l = dilated_window_attention(q, k, v, dilation, slopes)
        outs.append(o)
        lses.append(l)
    w = jax.nn.softmax(jnp.stack(lses, axis=0), axis=0)
    o = jnp.stack(outs, axis=0)
    return jnp.sum(w[..., None].astype(o.dtype) * o, axis=0)


def hybrid_layer(x, w_in, b_gate, g_pre, g_post, ln_g, ln_b, w_s, b_s, w_pa, w_pb, w_o):
    B, S, _ = x.shape
    h = rms_norm(x, g_pre)
    proj = h @ w_in
    u_a, v_a, z_a, q_b, k_b, v_b, z_b, gate_a, gate_b = jnp.split(proj, SPLIT_OFFS, axis=-1)
    y_a = spatial_gating(jax.nn.gelu(u_a), jax.nn.gelu(v_a), ln_g, ln_b, w_s, b_s) * jax.nn.silu(z_a)
    hs = (B, S, N_HEADS_B, HEAD_DIM_B)
    y_b = mixture_of_dilations(q_b.reshape(hs), k_b.reshape(hs), v_b.reshape(hs)).reshape(B, S, D_B)
    y_b = y_b * jax.nn.silu(z_b)
    g_a = jax.nn.sigmoid(gate_a + b_gate[:D_MODEL])
    g_b = jax.nn.sigmoid(gate_b + b_gate[D_MODEL:])
    m = g_a * (y_a @ w_pa) + g_b * (y_b @ w_pb)
    return x + rms_norm(m @ w_o, g_post)


def setup_inputs(seed: int = 0) -> dict:
    key = jax.random.key(seed)
    ks = jax.random.split(key, 14)
    f32 = jnp.float32
    nrm = lambda k, shape, scale: jax.random.normal(k, shape, f32) * scale
    return {
        "x": jax.random.normal(ks[0], (BATCH, SEQ, D_MODEL), f32),
        "w_in": nrm(ks[1], (DEPTH, D_MODEL, N_IN), D_MODEL ** -0.5),
        "b_gate": nrm(ks[2], (DEPTH, 2 * D_MODEL), 0.02),
        "g_pre": 1.0 + nrm(ks[3], (DEPTH, D_MODEL), 0.02),
        "g_post": 1.0 + nrm(ks[4], (DEPTH, D_MODEL), 0.02),
        "sgu_ln_g": 1.0 + nrm(ks[5], (DEPTH, D_A), 0.02),
        "sgu_ln_b": nrm(ks[6], (DEPTH, D_A), 0.02),
        "w_spatial": nrm(ks[7], (DEPTH, G_A, CHUNK, CHUNK), CHUNK ** -0.5),
        "b_spatial": 1.0 + nrm(ks[8], (DEPTH, G_A, CHUNK), 0.1),
        "w_proj_a": nrm(ks[9], (DEPTH, D_A, D_MODEL), D_A ** -0.5),
        "w_proj_b": nrm(ks[10], (DEPTH, D_B, D_MODEL), D_B ** -0.5),
        "w_out": nrm(ks[11], (DEPTH, D_MODEL, D_MODEL), D_MODEL ** -0.5),
    }


def reference(x, w_in, b_gate, g_pre, g_post, sgu_ln_g, sgu_ln_b, w_spatial, b_spatial,
              w_proj_a, w_proj_b, w_out):
    for l in range(DEPTH):
        x = hybrid_layer(x, w_in[l], b_gate[l], g_pre[l], g_post[l], sgu_ln_g[l], sgu_ln_b[l],
                         w_spatial[l], b_spatial[l], w_proj_a[l], w_proj_b[l], w_out[l])
    return x
```

```python
import numpy as np
import concourse.bass as bass
import concourse.mybir as mybir
from concourse.bass_utils import run_bass_kernel_spmd

F32 = mybir.dt.float32
BF16 = mybir.dt.bfloat16
AF = mybir.ActivationFunctionType
ALU = mybir.AluOpType

D = 1024
NIN = 12288
DA = 2048
SEQ = 4096
EPS = 1e-6
MW = 2560
SCALE = 128 ** -0.5
C_U, C_V, C_ZA, C_Q, C_K, C_VB, C_ZB, C_GA, C_GB = 0, 2048, 4096, 6144, 7168, 8192, 9216, 10240, 11264

ENGS = ("pe", "act", "dve", "pool", "sp")
SEM_LIMIT = 30000


class Sig:
    __slots__ = ("sem", "val", "eng")

    def __init__(self, sem, val, eng):
        self.sem, self.val, self.eng = sem, val, eng


class Tile:
    __slots__ = ("w", "rs")

    def __init__(self):
        self.w = None
        self.rs = {}


class Sched:
    def __init__(self, nc, sem_handles):
        self.nc = nc
        self.pool_sems = list(sem_handles)
        self.sems = []
        self.eng = {"pe": nc.tensor, "act": nc.scalar, "dve": nc.vector, "pool": nc.gpsimd, "sp": nc.sync}
        self.seen = {e: {} for e in ENGS}
        self.cur = {e: [self._alloc(), 0] for e in ("pe", "act", "dve", "pool")}
        self.rings = {"sp": [self._alloc() for _ in range(40)], "pool": [self._alloc() for _ in range(24)]}
        self.ring_i = {"sp": 0, "pool": 0}
        self.last = {}

    def _alloc(self):
        h = self.pool_sems.pop()
        self.sems.append(h)
        return len(self.sems) - 1

    def _emit_waits(self, eng, deps):
        best = {}
        for sg in deps:
            if sg is None:
                continue
            if sg.eng == "pe" and eng == "pe":
                continue
            if best.get(sg.sem, 0) < sg.val:
                best[sg.sem] = sg.val
        seen = self.seen[eng]
        for s, v in best.items():
            if seen.get(s, 0) >= v:
                continue
            seen[s] = v
            self.eng[eng].wait_ge(self.sems[s], v)

    def _deps(self, reads, writes):
        deps = []
        for t in reads:
            deps.append(t.w)
        for t in writes:
            deps.append(t.w)
            deps.extend(t.rs.values())
        return deps

    def op(self, eng, fns, reads=(), writes=()):
        if not isinstance(fns, (list, tuple)):
            fns = [fns]
        self._emit_waits(eng, self._deps(reads, writes))
        c = self.cur[eng]
        if c[1] >= SEM_LIMIT:
            c[0] = self._alloc()
            c[1] = 0
        c[1] += 1
        sig = Sig(c[0], c[1], eng)
        h = self.sems[c[0]]
        e = self.eng[eng]
        for f in fns[:-1]:
            f(e)
        fns[-1](e).then_inc(h, 1)
        for t in reads:
            t.rs[eng] = sig
        for t in writes:
            t.w = sig
            t.rs = {}
        self.last[eng] = sig
        return sig

    def dma(self, q, out_ap, in_ap, reads=(), writes=()):
        deps = self._deps(reads, writes)
        ring = self.rings[q]
        i = self.ring_i[q]
        self.ring_i[q] += 1
        s = ring[i % len(ring)]
        k = i // len(ring)
        if k > 0:
            deps.append(Sig(s, 16 * k, "dma"))
        self._emit_waits(q, deps)
        sig = Sig(s, 16 * (k + 1), "dma")
        h = self.sems[s]
        self.eng[q].dma_start(out=out_ap, in_=in_ap).then_inc(h, 16)
        for t in reads:
            t.rs[("d", s)] = sig
        for t in writes:
            t.w = sig
            t.rs = {}
        return sig

    def barrier(self):
        sigs = list(self.last.values())
        for q, ring in self.rings.items():
            n = self.ring_i[q]
            for idx, s in enumerate(ring):
                uses = (n - idx + len(ring) - 1) // len(ring) if n > idx else 0
                if uses > 0:
                    sigs.append(Sig(s, 16 * uses, "dma"))
        for e in ENGS:
            self._emit_waits(e, sigs)


def _build(layer_specs, n_x_rows, n_out_rows, debug=False):
    nc = bass.Bass("TRN2", target_bir_lowering=False)
    dt = nc.dram_tensor
    x_in = dt("x", [n_x_rows, D], F32, kind="ExternalInput").ap()
    w_in = dt("w_in", [2, D, NIN], F32, kind="ExternalInput").ap()
    w_pa = dt("w_pa", [2, DA, D], F32, kind="ExternalInput").ap()
    w_pb = dt("w_pb", [2, D, D], F32, kind="ExternalInput").ap()
    w_o = dt("w_o", [2, D, D], F32, kind="ExternalInput").ap()
    wsT_d = dt("wsT", [2, 16, 128, 128], F32, kind="ExternalInput").ap()
    bs_d = dt("bs", [2, 2048], F32, kind="ExternalInput").ap()
    gpre_d = dt("g_pre", [2, D], F32, kind="ExternalInput").ap()
    gpost_d = dt("g_post", [2, D], F32, kind="ExternalInput").ap()
    lng_d = dt("ln_g_t", [2, 128, 16], F32, kind="ExternalInput").ap()
    lnb_d = dt("ln_b_t", [2, 128, 16], F32, kind="ExternalInput").ap()
    bgt_d = dt("b_gate_t", [2, 128, 16], F32, kind="ExternalInput").ap()
    mask_d = dt("masks", [8, 128, MW], F32, kind="ExternalInput").ap()
    ident_d = dt("ident", [128, 128], F32, kind="ExternalInput").ap()
    out_d = dt("out", [n_out_rows, D], F32, kind="ExternalOutput").ap()

    skind = "ExternalOutput" if debug else "Internal"
    scr = []
    for li, (l, ntW, ntF) in enumerate(layer_specs):
        s = {}
        s["gv"] = dt(f"gv{li}", [ntF, DA], BF16, kind=skind).ap()
        s["uzT"] = dt(f"uzT{li}", [16, 128, ntF], BF16, kind=skind).ap()
        s["qT"] = dt(f"qT{li}", [8, 128, ntF], BF16, kind=skind).ap()
        s["kT"] = dt(f"kT{li}", [8, 128, ntW], BF16, kind=skind).ap()
        s["v1"] = dt(f"v1{li}", [8, 128, ntW // 128, 129], BF16, kind=skind).ap()
        s["zbT"] = dt(f"zbT{li}", [8, 128, ntF], BF16, kind=skind).ap()
        s["gT"] = dt(f"gT{li}", [16, 128, ntF], BF16, kind=skind).ap()
        s["ybT"] = dt(f"ybT{li}", [8, 128, ntF], BF16, kind=skind).ap()
        if li < len(layer_specs) - 1:
            s["xo"] = dt(f"xmid{li}", [ntF, D], F32, kind=skind).ap()
        else:
            s["xo"] = out_d
        scr.append(s)

    from contextlib import ExitStack
    with ExitStack() as top:
        sem_handles = [top.enter_context(nc.semaphore(f"s{i}")) for i in range(84)]
        S = Sched(nc, sem_handles)

        uid = [0]

        def sb(stack, name, shape, dtype):
            uid[0] += 1
            return stack.enter_context(nc.sbuf_tensor(f"sb{uid[0]}_{name}", shape, dtype))

        def ps(stack, name, shape, dtype):
            uid[0] += 1
            return stack.enter_context(nc.psum_tensor(f"ps{uid[0]}_{name}", shape, dtype))

        ident_f = sb(top, "ident_f", [128, 128], F32)
        ident = sb(top, "ident", [128, 128], BF16)
        ones_bf = sb(top, "ones_bf", [128, 128], BF16)
        T_ident_f, T_ident, T_ones = Tile(), Tile(), Tile()
        S.dma("sp", ident_f[:], ident_d[:, :], writes=[T_ident_f])
        S.op("pool", lambda e: e.tensor_copy(out=ident[:], in_=ident_f[:]), reads=[T_ident_f], writes=[T_ident])
        S.op("pool", lambda e: e.memset(ones_bf[:], 1.0), writes=[T_ones])

        x_src = x_in
        for li, (l, ntW, ntF) in enumerate(layer_specs):
            sc = scr[li]
            nTW, nTF = ntW // 128, ntF // 128
            nBF = ntF // 512
            nBW = ntW // 512
            dT = {}

            def dtile(*key):
                if key not in dT:
                    dT[key] = Tile()
                return dT[key]

            with ExitStack() as st:
                hT = sb(st, "hT", [128, 8, ntW], BF16)
                hT_t = [Tile() for _ in range(nTW)]
                gpre_bc = sb(st, "gpre_bc", [128, D], F32)
                T_gpre = Tile()
                S.dma("sp", gpre_bc[:], gpre_d[l].partition_broadcast(128), writes=[T_gpre])
                bgt = sb(st, "bgt", [128, 16], F32)
                T_bgt = Tile()
                S.dma("sp", bgt[:], bgt_d[l], writes=[T_bgt])
                if True:
                    s1 = st
                    wst = [sb(s1, f"wst{i}", [128, 8, 512], F32) for i in range(2)]
                    wb = [sb(s1, f"wb{i}", [128, 8, 512], BF16) for i in range(3)]
                    T_wst = [Tile() for _ in range(2)]
                    T_wb = [Tile() for _ in range(3)]
                    ust = sb(s1, "ust", [128, 4, ntF], BF16)
                    T_ust = {}
                    ost = [sb(s1, f"ost{i}", [128, 512], BF16) for i in range(4)]
                    T_ost = [Tile() for _ in range(4)]
                    zt = [sb(s1, f"zt{i}", [128, 512], BF16) for i in range(2)]
                    T_zt = [Tile() for _ in range(2)]
                    v1st = [sb(s1, f"v1st{i}", [128, 4, 129], BF16) for i in range(2)]
                    T_v1st = [Tile() for _ in range(2)]
                    for i in range(2):
                        S.op("pool", lambda e, i=i: e.memset(v1st[i][:], 1.0), writes=[T_v1st[i]])
                    acc = [ps(s1, f"acc{i}", [128, 512], F32) for i in range(4)]
                    T_acc = [Tile() for _ in range(4)]
                    cnt = {"wst": 0, "wb": 0, "acc": 0, "ost": 0, "zt": 0, "v1": 0}

                    T_wst_h = [[Tile(), Tile()] for _ in range(2)]
                    blk_order = ([C_K, C_K + 512, C_VB, C_VB + 512, C_Q, C_Q + 512, C_ZB, C_ZB + 512,
                                  C_GA, C_GA + 512, C_GB, C_GB + 512] + [C_V + 512 * i for i in range(4)])
                    for i in range(4):
                        blk_order += [C_U + 512 * i, C_ZA + 512 * i]
                    blk_slot = {}

                    def ensure_loaded(k):
                        if k >= len(blk_order) or k in blk_slot:
                            return
                        c0 = blk_order[k]
                        a = cnt["wst"] % 2
                        cnt["wst"] += 1
                        b = cnt["wb"] % 3
                        cnt["wb"] += 1
                        src = w_in[l, :, c0:c0 + 512].rearrange("(kc k) c -> k kc c", k=128)
                        S.dma("sp", wst[a][:, 0:4, :], src[:, 0:4, :], writes=[T_wst_h[a][0]])
                        S.dma("sp", wst[a][:, 4:8, :], src[:, 4:8, :], writes=[T_wst_h[a][1]])
                        S.op("dve", lambda e, a=a, b=b: e.tensor_copy(out=wb[b][:, 0:4, :], in_=wst[a][:, 0:4, :]),
                             reads=[T_wst_h[a][0]], writes=[T_wb[b]])
                        S.op("dve", lambda e, a=a, b=b: e.tensor_copy(out=wb[b][:, 4:8, :], in_=wst[a][:, 4:8, :]),
                             reads=[T_wst_h[a][1]], writes=[T_wb[b]])
                        blk_slot[k] = b

                    blk_next = [0]

                    def load_block2(c0):
                        k = blk_next[0]
                        blk_next[0] += 1
                        assert blk_order[k] == c0, (k, c0)
                        ensure_loaded(k)
                        ensure_loaded(k + 1)
                        return blk_slot[k]

                    def fm_mm(b, c, tb):
                        p = cnt["acc"] % 4
                        cnt["acc"] += 1
                        fns = [lambda e, p=p, b=b, c=c, tb=tb, kc=kc: e.matmul(
                            acc[p][:], lhsT=wb[b][:, kc, c * 128:(c + 1) * 128],
                            rhs=hT[:, kc, tb * 512:(tb + 1) * 512], start=(kc == 0), stop=(kc == 7)) for kc in range(8)]
                        S.op("pe", fns, reads=[T_wb[b]] + hT_t[tb * 4:tb * 4 + 4], writes=[T_acc[p]])
                        return p

                    def tm_mm(b, j):
                        p = cnt["acc"] % 4
                        cnt["acc"] += 1
                        fns = [lambda e, p=p, b=b, j=j, kc=kc: e.matmul(
                            acc[p][:], lhsT=hT[:, kc, j * 128:(j + 1) * 128],
                            rhs=wb[b][:, kc, :], start=(kc == 0), stop=(kc == 7)) for kc in range(8)]
                        S.op("pe", fns, reads=[T_wb[b], hT_t[j]], writes=[T_acc[p]])
                        return p

                    def fm_simple(c0, nblk_tok, func, dst, dkey, chunk0, bias_col0=None):
                        b = load_block2(c0)
                        for c in range(4):
                            for tb in range(nblk_tok):
                                p = fm_mm(b, c, tb)
                                o = cnt["ost"] % 4
                                cnt["ost"] += 1
                                if bias_col0 is None:
                                    S.op("act", lambda e, o=o, p=p: e.activation(out=ost[o][:], in_=acc[p][:], func=func),
                                         reads=[T_acc[p]], writes=[T_ost[o]])
                                else:
                                    bc = bias_col0 + c
                                    S.op("act", lambda e, o=o, p=p, bc=bc: e.activation(
                                        out=ost[o][:], in_=acc[p][:], func=func, bias=bgt[:, bc:bc + 1]),
                                         reads=[T_acc[p], T_bgt], writes=[T_ost[o]])
                                S.dma("pool", dst[chunk0 + c][:, tb * 512:(tb + 1) * 512], ost[o][:],
                                      reads=[T_ost[o]], writes=[dtile(dkey, chunk0 + c, tb)])

                    ensure_loaded(0)
                    ensure_loaded(1)
                if True:
                    s0 = st
                    NX = 6
                    xt = [sb(s0, f"xt{i}", [128, D], F32) for i in range(NX)]
                    xn = [sb(s0, f"xn{i}", [128, D], BF16) for i in range(4)]
                    junk = sb(s0, "junk0", [128, D], F32)
                    ssb = [sb(s0, f"ss{i}", [128, 4], F32) for i in range(2)]
                    pT = [ps(s0, f"pT{i}", [128, 8, 128], BF16) for i in range(4)]
                    T_xt = [Tile() for _ in range(NX)]
                    T_xn = [Tile() for _ in range(4)]
                    T_junk = Tile()
                    T_ss = [Tile() for _ in range(2)]
                    T_pT = [Tile() for _ in range(4)]
                    ssall = sb(s0, "ssall", [128, nTW], F32)
                    rstdall = sb(s0, "rstdall", [128, nTW], F32)
                    T_ssall, T_rstdall = Tile(), Tile()
                    xcnt = [0]
                    for j in range(nTW):
                        a = xcnt[0] % NX
                        xcnt[0] += 1
                        S.dma("sp", xt[a][:], x_src[j * 128:(j + 1) * 128, :], writes=[T_xt[a]])
                        S.op("act", lambda e, a=a, j=j: e.activation(out=junk[:], in_=xt[a][:], func=AF.Square,
                                                                     accum_out=ssall[:, j:j + 1]),
                             reads=[T_xt[a]], writes=[T_junk, T_ssall])
                    S.op("act", lambda e: e.activation(out=rstdall[:], in_=ssall[:], func=AF.Sqrt, bias=EPS, scale=1.0 / D),
                         reads=[T_ssall], writes=[T_rstdall])
                    S.op("dve", lambda e: e.reciprocal(out=rstdall[:], in_=rstdall[:]), writes=[T_rstdall])

                    def stage0_norm(j):
                        a = xcnt[0] % NX
                        xcnt[0] += 1
                        b = j % 4
                        S.dma("sp", xt[a][:], x_src[j * 128:(j + 1) * 128, :], writes=[T_xt[a]])
                        S.op("dve", lambda e, a=a, b=b, j=j: e.scalar_tensor_tensor(
                            out=xn[b][:], in0=xt[a][:], scalar=rstdall[:, j:j + 1], in1=gpre_bc[:],
                            op0=ALU.mult, op1=ALU.mult),
                             reads=[T_xt[a], T_rstdall, T_gpre], writes=[T_xn[b]])

                    def stage0_tr(j):
                        b = j % 4
                        fns = [lambda e, b=b, kc=kc: e.transpose(out=pT[b][:, kc, :], in_=xn[b][:, kc * 128:(kc + 1) * 128],
                                                                 identity=ident[:]) for kc in range(8)]
                        S.op("pe", fns, reads=[T_xn[b], T_ident], writes=[T_pT[b]])

                    def stage0_cp(j):
                        b = j % 4
                        S.op("dve", lambda e, b=b, j=j: e.tensor_copy(out=hT[:, :, j * 128:(j + 1) * 128], in_=pT[b][:]),
                             reads=[T_pT[b]], writes=[hT_t[j]])

                    kb = [load_block2(C_K), load_block2(C_K + 512)]
                    for fn_ in (stage0_norm, stage0_tr, stage0_cp):
                        for j in range(0, 4):
                            fn_(j)
                    for tb in range(nBW):
                        nxt_j = range(tb * 4 + 4, tb * 4 + 8) if tb + 1 < nBW else ()
                        for j in nxt_j:
                            stage0_norm(j)
                        for bb in range(2):
                            if bb == 1:
                                for j in nxt_j:
                                    stage0_tr(j)
                                for j in nxt_j:
                                    stage0_cp(j)
                            for c in range(4):
                                p = fm_mm(kb[bb], c, tb)
                                o = cnt["ost"] % 4
                                cnt["ost"] += 1
                                S.op("act", lambda e, o=o, p=p: e.activation(out=ost[o][:], in_=acc[p][:], func=AF.Copy),
                                     reads=[T_acc[p]], writes=[T_ost[o]])
                                S.dma("pool", sc["kT"][bb * 4 + c][:, tb * 512:(tb + 1) * 512], ost[o][:],
                                      reads=[T_ost[o]], writes=[dtile("kT", bb * 4 + c, tb)])
                    for bb in range(2):
                        b = load_block2(C_VB + bb * 512)
                        for j in range(nTW):
                            p = tm_mm(b, j)
                            o = cnt["v1"] % 2
                            cnt["v1"] += 1
                            S.op("act", lambda e, o=o, p=p: e.activation(
                                out=v1st[o][:, :, 0:128], in_=acc[p][:].rearrange("p (h e) -> p h e", e=128), func=AF.Copy),
                                 reads=[T_acc[p]], writes=[T_v1st[o]])
                            S.dma("pool", sc["v1"][bb * 4:(bb + 1) * 4, :, j, :].rearrange("h p e -> p h e"), v1st[o][:],
                                  reads=[T_v1st[o]], writes=[dtile("v1", bb, j)])
                    for bb in range(2):
                        fm_simple(C_Q + bb * 512, nBF, AF.Copy, sc["qT"], "qT", bb * 4)
                    for bb in range(2):
                        fm_simple(C_ZB + bb * 512, nBF, AF.Silu, sc["zbT"], "zbT", bb * 4)
                    for bb in range(2):
                        fm_simple(C_GA + bb * 512, nBF, AF.Sigmoid, sc["gT"], "gT", bb * 4, bias_col0=bb * 4)
                    for bb in range(2):
                        fm_simple(C_GB + bb * 512, nBF, AF.Sigmoid, sc["gT"], "gT", 8 + bb * 4, bias_col0=8 + bb * 4)
                    for bb in range(4):
                        b = load_block2(C_V + bb * 512)
                        for j in range(nTF):
                            p = tm_mm(b, j)
                            o = cnt["ost"] % 4
                            cnt["ost"] += 1
                            S.op("act", lambda e, o=o, p=p: e.activation(out=ost[o][:], in_=acc[p][:], func=AF.Gelu_apprx_tanh),
                                 reads=[T_acc[p]], writes=[T_ost[o]])
                            S.dma("pool", sc["gv"][j * 128:(j + 1) * 128, bb * 512:(bb + 1) * 512], ost[o][:],
                                  reads=[T_ost[o]], writes=[dtile("gv", bb, j)])
                    for bb in range(4):
                        b = load_block2(C_U + bb * 512)
                        for c in range(4):
                            for tb in range(nBF):
                                p = fm_mm(b, c, tb)
                                tk = (c, tb)
                                if tk not in T_ust:
                                    T_ust[tk] = Tile()
                                S.op("act", lambda e, c=c, tb=tb, p=p: e.activation(
                                    out=ust[:, c, tb * 512:(tb + 1) * 512], in_=acc[p][:], func=AF.Gelu_apprx_tanh),
                                     reads=[T_acc[p]], writes=[T_ust[tk]])
                        b = load_block2(C_ZA + bb * 512)
                        for c in range(4):
                            for tb in range(nBF):
                                p = fm_mm(b, c, tb)
                                z = cnt["zt"] % 2
                                cnt["zt"] += 1
                                S.op("act", lambda e, z=z, p=p: e.activation(out=zt[z][:], in_=acc[p][:], func=AF.Silu),
                                     reads=[T_acc[p]], writes=[T_zt[z]])
                                o = cnt["ost"] % 4
                                cnt["ost"] += 1
                                S.op("dve", lambda e, o=o, z=z, c=c, tb=tb: e.tensor_tensor(
                                    out=ost[o][:], in0=zt[z][:], in1=ust[:, c, tb * 512:(tb + 1) * 512], op=ALU.mult),
                                     reads=[T_zt[z], T_ust[(c, tb)]], writes=[T_ost[o]])
                                S.dma("pool", sc["uzT"][bb * 4 + c][:, tb * 512:(tb + 1) * 512], ost[o][:],
                                      reads=[T_ost[o]], writes=[dtile("uzT", bb * 4 + c, tb)])
                S.barrier()

            with ExitStack() as st:
                wpa = sb(st, "wpa", [128, 16, D], BF16)
                wpb = sb(st, "wpb", [128, 8, D], BF16)
                wo = sb(st, "wo", [128, 8, D], BF16)
                T_wpa = [Tile() for _ in range(16)]
                T_wpb = [Tile() for _ in range(8)]
                T_wo = [Tile() for _ in range(8)]
                wsT_bf = sb(st, "wsT_bf", [128, 16, 128], BF16)
                T_wsT = Tile()
                biasT = sb(st, "biasT", [128, 16, 2, 128], F32)
                T_biasT = Tile()
                lng = sb(st, "lng", [128, 16], F32)
                lnb = sb(st, "lnb", [128, 16], F32)
                T_lng, T_lnb = Tile(), Tile()
                gpost_bc = sb(st, "gpost_bc", [128, D], F32)
                T_gpost = Tile()
                S.dma("sp", lng[:], lng_d[l], writes=[T_lng])
                S.dma("sp", lnb[:], lnb_d[l], writes=[T_lnb])
                S.dma("sp", gpost_bc[:], gpost_d[l].partition_broadcast(128), writes=[T_gpost])

                with ExitStack() as s2:
                    wst2 = [sb(s2, f"wst2_{i}", [128, 2, D], F32) for i in range(2)]
                    T_wst2 = [Tile() for _ in range(2)]
                    wcnt = [0]

                    def load_proj(src_rows_ap, dst, dst_tiles, kc0, nkc):
                        a = wcnt[0] % 2
                        wcnt[0] += 1
                        S.dma("sp", wst2[a][:, 0:nkc, :], src_rows_ap.rearrange("(kc k) c -> k kc c", k=128),
                              writes=[T_wst2[a]])
                        S.op("pool", lambda e, a=a: e.tensor_copy(out=dst[:, kc0:kc0 + nkc, :], in_=wst2[a][:, 0:nkc, :]),
                             reads=[T_wst2[a]], writes=dst_tiles[kc0:kc0 + nkc])

                    proj_jobs = []
                    for q4 in range(8):
                        proj_jobs.append((w_pa[l, q4 * 256:(q4 + 1) * 256, :], wpa, T_wpa, q4 * 2, 2))
                    for q4 in range(4):
                        proj_jobs.append((w_pb[l, q4 * 256:(q4 + 1) * 256, :], wpb, T_wpb, q4 * 2, 2))
                    for q4 in range(4):
                        proj_jobs.append((w_o[l, q4 * 256:(q4 + 1) * 256, :], wo, T_wo, q4 * 2, 2))

                    wsT_f = wst2[0][:].rearrange("p a (g q) -> p (a g) q", q=128)
                    bs_bc = wst2[1][:].rearrange("p a b -> p (a b)")
                    T_wsTf, T_bsbc = T_wst2[0], T_wst2[1]
                    S.dma("sp", wsT_f, wsT_d[l].rearrange("g q p -> q g p"), writes=[T_wsTf])
                    S.dma("sp", bs_bc, bs_d[l].partition_broadcast(128), writes=[T_bsbc])
                    S.op("pool", lambda e: e.tensor_copy(out=wsT_bf[:], in_=wsT_f), reads=[T_wsTf], writes=[T_wsT])

                    kTh = [sb(s2, f"kTh{i}", [128, ntW], BF16) for i in range(2)]
                    v1h = [sb(s2, f"v1h{i}", [128, nTW, 129], BF16) for i in range(2)]
                    qTh = [sb(s2, f"qTh{i}", [128, ntF], BF16) for i in range(2)]
                    zbh = [sb(s2, f"zbh{i}", [128, ntF], BF16) for i in range(2)]
                    mkf = [sb(s2, "mkf0", [128, MW], F32)]
                    mk = [sb(s2, f"mk{i}", [128, MW], BF16) for i in range(2)]
                    T_kTh = [Tile() for _ in range(2)]
                    T_v1h = [Tile() for _ in range(2)]
                    T_qTh = [Tile() for _ in range(2)]
                    T_zbh = [Tile() for _ in range(2)]
                    T_mkf = [Tile()]
                    T_mk = [Tile() for _ in range(2)]
                    LOOK = 3
                    NS = LOOK + 1
                    NE = LOOK + 2
                    et = [sb(s2, f"et{i}", [128, 512], BF16) for i in range(NE)]
                    pt = [sb(s2, f"pt{i}", [128, 512], BF16) for i in range(NE)]
                    T_et = [Tile() for _ in range(NE)]
                    T_pt = [Tile() for _ in range(NE)]
                    on_t = [sb(s2, f"on{i}", [128, 128], BF16) for i in range(2)]
                    T_on = [Tile() for _ in range(2)]
                    rden = [sb(s2, f"rden{i}", [128, 2], F32) for i in range(2)]
                    T_rden = [Tile() for _ in range(2)]
                    ybst = [sb(s2, f"ybst{i}", [128, 512], BF16) for i in range(2)]
                    T_ybst = [Tile() for _ in range(2)]
                    ybc = [sb(s2, f"ybc{i}", [128, 512], BF16) for i in range(2)]
                    T_ybc = [Tile() for _ in range(2)]
                    stp = [ps(s2, f"stp{i}", [128, 512], F32) for i in range(NS)]
                    ops_ = [ps(s2, f"ops{i}", [128, 512], F32) for i in range(2)]
                    tpp = [ps(s2, f"tpp{i}", [128, 1024], BF16) for i in range(2)]
                    T_stp = [Tile() for _ in range(NS)]
                    T_ops = [Tile() for _ in range(2)]
                    T_tpp = [Tile() for _ in range(2)]

                    for g4 in range(4):
                        pslot = g4 % 2
                        fns = [lambda e, pslot=pslot, g=g4 * 4 + gi, gi=gi: e.matmul(
                            stp[pslot][:, gi * 128:(gi + 1) * 128], lhsT=ones_bf[:], rhs=wsT_bf[:, g, :],
                            start=True, stop=True) for gi in range(4)]
                        S.op("pe", fns, reads=[T_ones, T_wsT], writes=[T_stp[pslot]])
                        for gi in range(4):
                            g = g4 * 4 + gi
                            for cdup in range(2):
                                S.op("dve", lambda e, pslot=pslot, g=g, gi=gi, cdup=cdup: e.scalar_tensor_tensor(
                                    out=biasT[:, g, cdup, :], in0=stp[pslot][:, gi * 128:(gi + 1) * 128], scalar=lnb[:, g:g + 1],
                                    in1=bs_bc[:, g * 128:(g + 1) * 128], op0=ALU.mult, op1=ALU.add),
                                     reads=[T_stp[pslot], T_lnb, T_bsbc], writes=[T_biasT])

                    def load_head(h):
                        s_ = h % 2
                        S.dma("sp", kTh[s_][:], sc["kT"][h], reads=[dtile("kT", h, tb) for tb in range(nBW)], writes=[T_kTh[s_]])
                        S.dma("sp", v1h[s_][:], sc["v1"][h], reads=[dtile("v1", h // 4, j) for j in range(nTW)], writes=[T_v1h[s_]])
                        S.dma("sp", qTh[s_][:], sc["qT"][h], reads=[dtile("qT", h, tb) for tb in range(nBF)], writes=[T_qTh[s_]])
                        S.dma("sp", zbh[s_][:], sc["zbT"][h], reads=[dtile("zbT", h, tb) for tb in range(nBF)], writes=[T_zbh[s_]])
                        S.dma("sp", mkf[0][:], mask_d[h], writes=[T_mkf[0]])
                        S.op("pool", lambda e, s_=s_: e.tensor_copy(out=mk[s_][:], in_=mkf[0][:]), reads=[T_mkf[0]], writes=[T_mk[s_]])

                    load_head(0)
                    tix = [0]
                    for h in range(8):
                        hs = h % 2
                        if h + 1 < 8:
                            load_head(h + 1)
                        for _ in range(2):
                            if proj_jobs:
                                load_proj(*proj_jobs.pop(0))
                        R = HEAD_RADIUS[h]
                        groups = []
                        for i in range(nTF):
                            j_hi = min(nTW - 1, i + R)
                            j_lo = max(0, i - R)
                            js = list(range(j_hi, j_lo - 1, -1))
                            gl = [js[a:a + 4] for a in range(0, len(js), 4)]
                            for gi, g in enumerate(gl):
                                groups.append((i, g, gi == 0, gi == len(gl) - 1, tix[0]))
                            tix[0] += 1
                        G = len(groups)

                        def emit_qk(idx):
                            i, js, first, last, tx = groups[idx]
                            n = len(js)
                            ss_, es = idx % NS, idx % NE
                            fns = [lambda e, ss_=ss_, m=m, j=j, i=i: e.matmul(
                                stp[ss_][:, m * 128:(m + 1) * 128], lhsT=kTh[hs][:, j * 128:(j + 1) * 128],
                                rhs=qTh[hs][:, i * 128:(i + 1) * 128], start=True, stop=True) for m, j in enumerate(js)]
                            S.op("pe", fns, reads=[T_kTh[hs], T_qTh[hs]], writes=[T_stp[ss_]])
                            S.op("act", lambda e, ss_=ss_, es=es, n=n: e.activation(
                                out=et[es][:, 0:n * 128], in_=stp[ss_][:, 0:n * 128], func=AF.Exp, scale=SCALE),
                                 reads=[T_stp[ss_]], writes=[T_et[es]])
                            x0 = 1024 - (js[0] - i) * 128
                            S.op("dve", lambda e, es=es, n=n, x0=x0: e.tensor_tensor(
                                out=pt[es][:, 0:n * 128], in0=et[es][:, 0:n * 128], in1=mk[hs][:, x0:x0 + n * 128], op=ALU.mult),
                                 reads=[T_et[es], T_mk[hs]], writes=[T_pt[es]])

                        def emit_pv(idx):
                            i, js, first, last, tx = groups[idx]
                            es = idx % NE
                            osl = tx % 2
                            n = len(js)
                            fns = [lambda e, osl=osl, es=es, m=m, j=j, first=first, last=last, n=n: e.matmul(
                                ops_[osl][:, 0:129], lhsT=pt[es][:, m * 128:(m + 1) * 128], rhs=v1h[hs][:, j, :],
                                start=(first and m == 0), stop=(last and m == n - 1)) for m, j in enumerate(js)]
                            S.op("pe", fns, reads=[T_pt[es], T_v1h[hs]], writes=[T_ops[osl]])

                        def emit_fin_dve(idx):
                            i, js, first, last, tx = groups[idx]
                            osl = tx % 2
                            S.op("dve", lambda e, osl=osl: e.reciprocal(out=rden[osl][:, 0:1], in_=ops_[osl][:, 128:129]),
                                 reads=[T_ops[osl]], writes=[T_rden[osl]])

                        def emit_fin_act(idx):
                            i, js, first, last, tx = groups[idx]
                            osl = tx % 2
                            S.op("dve", lambda e, osl=osl: e.tensor_scalar(
                                out=on_t[osl][:], in0=ops_[osl][:, 0:128], scalar1=rden[osl][:, 0:1], scalar2=None,
                                op0=ALU.mult),
                                 reads=[T_ops[osl], T_rden[osl]], writes=[T_on[osl]])

                        def emit_fin_pe(idx):
                            i, js, first, last, tx = groups[idx]
                            osl = tx % 2
                            c = i % 4
                            tb = i // 4
                            tsl = tb % 2
                            S.op("pe", lambda e, osl=osl, tsl=tsl, c=c: e.transpose(
                                out=tpp[tsl][:, c * 128:(c + 1) * 128], in_=on_t[osl][:], identity=ident[:]),
                                 reads=[T_on[osl], T_ident], writes=[T_tpp[tsl]])

                        def emit_fin_yb(idx):
                            i, js, first, last, tx = groups[idx]
                            c = i % 4
                            tb = i // 4
                            tsl = tb % 2
                            if c == 3:
                                S.op("dve", lambda e, tsl=tsl: e.tensor_copy(out=ybc[tsl][:], in_=tpp[tsl][:, 0:512]),
                                     reads=[T_tpp[tsl]], writes=[T_ybc[tsl]])
                                S.op("pool", lambda e, tsl=tsl, tb=tb: e.tensor_tensor(
                                    out=ybst[tsl][:], in0=ybc[tsl][:], in1=zbh[hs][:, tb * 512:(tb + 1) * 512],
                                    op=ALU.mult),
                                     reads=[T_ybc[tsl], T_zbh[hs]], writes=[T_ybst[tsl]])
                                S.dma("pool", sc["ybT"][h][:, tb * 512:(tb + 1) * 512], ybst[tsl][:],
                                      reads=[T_ybst[tsl]], writes=[dtile("ybT", h, tb)])

                        for idx in range(G + LOOK + 6):
                            for dk, fn in ((1, emit_fin_dve), (2, emit_fin_act), (3, emit_fin_pe), (5, emit_fin_yb)):
                                k = idx - LOOK - dk
                                if 0 <= k < G and groups[k][3]:
                                    fn(k)
                            if idx < G:
                                emit_qk(idx)
                            k = idx - LOOK
                            if 0 <= k < G:
                                emit_pv(k)
                    while proj_jobs:
                        load_proj(*proj_jobs.pop(0))
                S.barrier()

                with ExitStack() as s3:
                    TB = 256
                    gvt = [sb(s3, "gvt0", [128, 2, DA], BF16)]
                    uzt = [sb(s3, f"uzt{i}", [128, 16, TB], BF16) for i in range(2)]
                    gt_ = [sb(s3, f"gtt{i}", [128, 16, TB], BF16) for i in range(2)]
                    ybt = [sb(s3, f"ybt{i}", [128, 8, TB], BF16) for i in range(2)]
                    xrt = [sb(s3, f"xrt{i}", [128, 2, D], F32) for i in range(2)]
                    T_gvt = [Tile()]
                    T_uzt = [[Tile() for _ in range(16)] for _ in range(2)]
                    T_gt = [Tile() for _ in range(2)]
                    T_ybt = [Tile() for _ in range(2)]
                    T_xrt = [Tile() for _ in range(2)]
                    vhat = [sb(s3, f"vhat{i}", [128, 2, DA], BF16) for i in range(2)]
                    T_vhat = [[Tile(), Tile()] for _ in range(2)]
                    lnst = sb(s3, "lnst", [128, 2, 8], F32)
                    T_lnst = [Tile(), Tile()]
                    junkb = sb(s3, "junkb", [128, DA], BF16)
                    T_junkb = Tile()
                    neghalf = sb(s3, "neghalf", [128, 2], F32)
                    T_neghalf = Tile()
                    S.op("pool", lambda e: e.memset(neghalf[:], -0.5), writes=[T_neghalf])
                    tmp = [sb(s3, f"tmp{i}", [128, TB], F32) for i in range(2)]
                    T_tmp = [Tile() for _ in range(2)]
                    t1 = [sb(s3, f"t1_{i}", [128, TB], F32) for i in range(2)]
                    T_t1 = [Tile() for _ in range(2)]
                    t2all = sb(s3, "t2all", [128, 8, TB], F32)
                    T_t2 = [Tile() for _ in range(8)]
                    mT = sb(s3, "mT", [128, 8, TB], BF16)
                    T_mT = [Tile() for _ in range(8)]
                    junk3 = sb(s3, "junk3", [128, 512], F32)
                    T_junk3 = Tile()
                    ssr = [sb(s3, f"ssr{i}", [128, 4], F32) for i in range(2)]
                    T_ssr = [Tile() for _ in range(2)]
                    yt = [sb(s3, f"yt{i}", [128, D], F32) for i in range(2)]
                    T_yt = [Tile() for _ in range(2)]
                    ot = [sb(s3, f"ot{i}", [128, D], F32) for i in range(2)]
                    T_ot = [Tile() for _ in range(2)]
                    spp = [ps(s3, f"spp{i}", [128, 512], F32) for i in range(2)]
                    pbp = [ps(s3, "pbp0", [128, 512], F32)]
                    pap = [ps(s3, f"pap{i}", [128, 512], F32) for i in range(2)]
                    rp = [ps(s3, f"rp{i}", [128, 512], F32) for i in range(3)]
                    T_spp = [Tile() for _ in range(2)]
                    T_pap = [Tile() for _ in range(2)]
                    T_pbp = [Tile()]
                    T_rp = [Tile() for _ in range(3)]
                    nblk = ntF // TB
                    c3 = {"sp": 0, "pa": 0, "pb": 0, "y": 0, "rp": 0}

                    def load3a(bi):
                        t0 = bi * TB
                        S.dma("sp", gvt[0][:], sc["gv"][t0:t0 + TB, :].rearrange("(c p) f -> p c f", p=128),
                              reads=[dtile("gv", bb, t0 // 128 + cc) for bb in range(4) for cc in range(2)], writes=[T_gvt[0]])

                    def load3b(bi):
                        s_ = bi % 2
                        t0 = bi * TB
                        tb512 = t0 // 512
                        S.dma("sp", uzt[s_][:], sc["uzT"][:, :, t0:t0 + TB].rearrange("g e t -> e g t"),
                              reads=[dtile("uzT", g, tb512) for g in range(16)], writes=T_uzt[s_])
                        S.dma("sp", ybt[s_][:], sc["ybT"][:, :, t0:t0 + TB].rearrange("g e t -> e g t"),
                              reads=[dtile("ybT", g, tb512) for g in range(8)], writes=[T_ybt[s_]])
                        S.dma("sp", gt_[s_][:], sc["gT"][:, :, t0:t0 + TB].rearrange("g e t -> e g t"),
                              reads=[dtile("gT", g, tb512) for g in range(16)], writes=[T_gt[s_]])
                        S.dma("sp", xrt[s_][:], x_src[t0:t0 + TB, :].rearrange("(c p) f -> p c f", p=128), writes=[T_xrt[s_]])

                    def ln_sums(bi):
                        for c in range(2):
                            S.op("act", lambda e, c=c: e.activation(out=junkb[:], in_=gvt[0][:, c, :], func=AF.Copy,
                                                                    accum_out=lnst[:, c, 0:1]),
                                 reads=[T_gvt[0]], writes=[T_junkb, T_lnst[c]])
                            S.op("act", lambda e, c=c: e.activation(out=junkb[:], in_=gvt[0][:, c, :], func=AF.Square,
                                                                    accum_out=lnst[:, c, 1:2]),
                                 reads=[T_gvt[0]], writes=[T_junkb, T_lnst[c]])

                    def ln_small(bi):
                        for c in range(2):
                            ops_l = [
                                lambda e, c=c: e.tensor_scalar(out=lnst[:, c, 2:3], in0=lnst[:, c, 0:1], scalar1=-1.0 / DA, scalar2=None, op0=ALU.mult),
                                lambda e, c=c: e.tensor_tensor(out=lnst[:, c, 3:4], in0=lnst[:, c, 2:3], in1=lnst[:, c, 2:3], op=ALU.mult),
                                lambda e, c=c: e.tensor_scalar(out=lnst[:, c, 4:5], in0=lnst[:, c, 1:2], scalar1=1.0 / DA, scalar2=EPS, op0=ALU.mult, op1=ALU.add),
                                lambda e, c=c: e.tensor_tensor(out=lnst[:, c, 5:6], in0=lnst[:, c, 4:5], in1=lnst[:, c, 3:4], op=ALU.subtract),
                                lambda e, c=c: e.tensor_tensor(out=lnst[:, c, 6:7], in0=lnst[:, c, 5:6], in1=neghalf[:, 0:1], op=ALU.pow),
                                lambda e, c=c: e.tensor_tensor(out=lnst[:, c, 7:8], in0=lnst[:, c, 2:3], in1=lnst[:, c, 6:7], op=ALU.mult),
                            ]
                            for f in ops_l:
                                S.op("pool", f, reads=[T_neghalf], writes=[T_lnst[c]])

                    def ln_vhat(bi):
                        vs = bi % 2
                        for c in range(2):
                            S.op("act", lambda e, c=c, vs=vs: e.activation(
                                out=vhat[vs][:, c, :], in_=gvt[0][:, c, :], func=AF.Identity,
                                scale=lnst[:, c, 6:7], bias=lnst[:, c, 7:8]),
                                 reads=[T_gvt[0], T_lnst[c]], writes=[T_vhat[vs][c]])

                    def phase_spatial(bi):
                        s_, vs = bi % 2, bi % 2

                        def pool_mult(p, g):
                            S.op("pool", lambda e, p=p, g=g: e.tensor_tensor(
                                out=uzt[s_][:, g, :], in0=tmp[p][:], in1=uzt[s_][:, g, :], op=ALU.mult),
                                 reads=[T_tmp[p]], writes=[T_uzt[s_][g]])

                        pend = None
                        for g in range(16):
                            p = c3["sp"] % 2
                            c3["sp"] += 1
                            fns = [lambda e, p=p, c=c, g=g: e.matmul(
                                spp[p][:, c * 128:(c + 1) * 128], lhsT=vhat[vs][:, c, g * 128:(g + 1) * 128],
                                rhs=wsT_bf[:, g, :], start=True, stop=True) for c in range(2)]
                            S.op("pe", fns, reads=[T_vhat[vs][0], T_vhat[vs][1], T_wsT], writes=[T_spp[p]])
                            S.op("dve", lambda e, p=p, g=g: e.scalar_tensor_tensor(
                                out=tmp[p][:], in0=spp[p][:, 0:TB], scalar=lng[:, g:g + 1],
                                in1=biasT[:, g, :, :].rearrange("p a b -> p (a b)"), op0=ALU.mult, op1=ALU.add),
                                 reads=[T_spp[p], T_lng, T_biasT], writes=[T_tmp[p]])
                            if pend is not None:
                                pool_mult(*pend)
                            pend = (p, g)
                        pool_mult(*pend)

                    def phase_pb(bi, jcs):
                        s_ = bi % 2

                        def pb(jc):
                            pq = 0
                            fns = [lambda e, pq=pq, hh=hh, jc=jc: e.matmul(
                                pbp[0][:, 0:TB], lhsT=wpb[:, hh, jc * 128:(jc + 1) * 128], rhs=ybt[s_][:, hh, :],
                                start=(hh == 0), stop=(hh == 7)) for hh in range(8)]
                            S.op("pe", fns, reads=T_wpb + [T_ybt[s_]], writes=[T_pbp[pq]])
                            S.op("dve", lambda e, pq=pq, jc=jc: e.tensor_tensor(
                                out=t2all[:, jc, :], in0=pbp[0][:, 0:TB], in1=gt_[s_][:, 8 + jc, :], op=ALU.mult),
                                 reads=[T_pbp[pq], T_gt[s_]], writes=[T_t2[jc]])

                        for jc in jcs:
                            pb(jc)

                    def phase_pa(bi):
                        s_ = bi % 2
                        for jc in range(8):
                            p = c3["pa"] % 2
                            c3["pa"] += 1
                            fns = [lambda e, p=p, g=g, jc=jc: e.matmul(
                                pap[p][:, 0:TB], lhsT=wpa[:, g, jc * 128:(jc + 1) * 128], rhs=uzt[s_][:, g, :],
                                start=(g == 0), stop=(g == 15)) for g in range(16)]
                            S.op("pe", fns, reads=T_wpa + T_uzt[s_], writes=[T_pap[p]])
                            S.op("dve", lambda e, p=p, jc=jc: e.tensor_tensor(
                                out=t1[p][:], in0=pap[p][:, 0:TB], in1=gt_[s_][:, jc, :], op=ALU.mult),
                                 reads=[T_pap[p], T_gt[s_]], writes=[T_t1[p]])
                            S.op("pool", lambda e, p=p, jc=jc: e.tensor_tensor(
                                out=mT[:, jc, :], in0=t1[p][:], in1=t2all[:, jc, :], op=ALU.add),
                                 reads=[T_t1[p], T_t2[jc]], writes=[T_mT[jc]])

                    def phase_out(bi):
                        s_ = bi % 2
                        t0 = bi * TB
                        for c in range(2):
                            yi = c3["y"] % 2
                            c3["y"] += 1
                            rbs = [(c3["rp"] + hf) % 3 for hf in range(2)]
                            c3["rp"] += 2
                            for hf in range(2):
                                rb = rbs[hf]
                                fns = [lambda e, rb=rb, hf=hf, k=k, c=c: e.matmul(
                                    rp[rb][:], lhsT=mT[:, k, c * 128:(c + 1) * 128], rhs=wo[:, k, hf * 512:(hf + 1) * 512],
                                    start=(k == 0), stop=(k == 7)) for k in range(8)]
                                S.op("pe", fns, reads=T_mT + T_wo, writes=[T_rp[rb]])
                                S.op("act", lambda e, rb=rb, hf=hf, yi=yi: e.activation(
                                    out=junk3[:], in_=rp[rb][:], func=AF.Square, accum_out=ssr[yi][:, hf:hf + 1]),
                                     reads=[T_rp[rb]], writes=[T_junk3, T_ssr[yi]])
                            if bi + 1 < nblk:
                                phase_pb(bi + 1, range(c * 4, c * 4 + 4))
                            S.op("dve", lambda e, yi=yi: e.tensor_tensor(out=ssr[yi][:, 2:3], in0=ssr[yi][:, 0:1], in1=ssr[yi][:, 1:2], op=ALU.add),
                                 writes=[T_ssr[yi]])
                            S.op("act", lambda e, yi=yi: e.activation(out=ssr[yi][:, 3:4], in_=ssr[yi][:, 2:3], func=AF.Sqrt, bias=EPS, scale=1.0 / D),
                                 writes=[T_ssr[yi]])
                            S.op("dve", lambda e, yi=yi: e.reciprocal(out=ssr[yi][:, 2:3], in_=ssr[yi][:, 3:4]), writes=[T_ssr[yi]])
                            for hf in range(2):
                                rb = rbs[hf]
                                S.op("dve", lambda e, rb=rb, hf=hf, yi=yi: e.scalar_tensor_tensor(
                                    out=yt[yi][:, hf * 512:(hf + 1) * 512], in0=rp[rb][:], scalar=ssr[yi][:, 2:3],
                                    in1=gpost_bc[:, hf * 512:(hf + 1) * 512], op0=ALU.mult, op1=ALU.mult),
                                     reads=[T_rp[rb], T_ssr[yi], T_gpost], writes=[T_yt[yi]])
                            S.op("pool", lambda e, yi=yi, c=c: e.tensor_tensor(
                                out=ot[yi][:], in0=yt[yi][:], in1=xrt[s_][:, c, :], op=ALU.add),
                                 reads=[T_yt[yi], T_xrt[s_]], writes=[T_ot[yi]])
                            r0 = t0 + c * 128
                            S.dma("pool", sc["xo"][r0:r0 + 128, :], ot[yi][:], reads=[T_ot[yi]],
                                  writes=[dtile("xo", r0 // 128)])

                    load3a(0)
                    load3b(0)
                    ln_sums(0)
                    ln_small(0)
                    ln_vhat(0)
                    phase_pb(0, range(8))
                    for bi in range(nblk):
                        nxt = bi + 1 < nblk
                        if nxt:
                            load3a(bi + 1)
                            load3b(bi + 1)
                            ln_sums(bi + 1)
                        phase_spatial(bi)
                        if nxt:
                            ln_small(bi + 1)
                            ln_vhat(bi + 1)
                        phase_pa(bi)
                        phase_out(bi)
                S.barrier()
            x_src = sc["xo"]

        S.barrier()
    return nc


def _mask_table():
    p = np.arange(128)[:, None]
    xx = np.arange(MW)[None, :]
    d = (p - xx + 1024).astype(np.int64)
    ad = np.abs(d)
    mult = (ad <= 64).astype(np.float64) + ((ad <= 256) & (d % 4 == 0)) + ((ad <= 1024) & (d % 16 == 0))
    slopes = 2.0 ** (-8.0 * (np.arange(8) + 1.0) / 8.0)
    T = mult[None] * np.exp(-slopes[:, None, None] * ad[None].astype(np.float64))
    T[T < 1e-37] = 0.0
    return np.ascontiguousarray(T.astype(np.float32))


def _head_radius():
    T = _mask_table()
    d = (np.arange(128)[:, None] - np.arange(MW)[None, :] + 1024)
    out = []
    for h in range(8):
        nz = np.abs(d[T[h] != 0.0])
        dmax = int(nz.max())
        out.append(min(8, (dmax + 127) // 128))
    return out


_CONST = {}
HEAD_RADIUS = None


def _consts():
    if not _CONST:
        _CONST["masks"] = _mask_table()
        _CONST["ident"] = np.eye(128, dtype=np.float32)
    return _CONST


def _core_inputs(c, x_local, w_in, b_gate, g_pre, g_post, sgu_ln_g, sgu_ln_b, w_spatial, b_spatial,
                 w_proj_a, w_proj_b, w_out):
    rev = (c % 2 == 1)
    ws = w_spatial[:, :, ::-1, ::-1] if rev else w_spatial
    bs = b_spatial[:, :, ::-1] if rev else b_spatial
    cst = _consts()
    f = np.ascontiguousarray
    return {
        "x": f(x_local),
        "w_in": f(w_in), "w_pa": f(w_proj_a), "w_pb": f(w_proj_b), "w_o": f(w_out),
        "wsT": f(np.transpose(ws, (0, 1, 3, 2))),
        "bs": f(bs.reshape(2, 2048)),
        "g_pre": f(g_pre), "g_post": f(g_post),
        "ln_g_t": f(sgu_ln_g.reshape(2, 16, 128).transpose(0, 2, 1)),
        "ln_b_t": f(sgu_ln_b.reshape(2, 16, 128).transpose(0, 2, 1)),
        "b_gate_t": f(b_gate.reshape(2, 16, 128).transpose(0, 2, 1)),
        "masks": cst["masks"], "ident": cst["ident"],
    }


_NC_CACHE = {}
FUSED = True


def _get_nc(key, *args, **kw):
    global HEAD_RADIUS
    if HEAD_RADIUS is None:
        HEAD_RADIUS = _head_radius()
    if key not in _NC_CACHE:
        _NC_CACHE[key] = _build(*args, **kw)
    return _NC_CACHE[key]


def kernel(x, w_in, b_gate, g_pre, g_post, sgu_ln_g, sgu_ln_b, w_spatial, b_spatial,
           w_proj_a, w_proj_b, w_out):
    arrs = [np.asarray(a, dtype=np.float32) for a in
            (x, w_in, b_gate, g_pre, g_post, sgu_ln_g, sgu_ln_b, w_spatial, b_spatial, w_proj_a, w_proj_b, w_out)]
    x = arrs[0]
    rest = arrs[1:]
    B = x.shape[0]
    xl = []
    for c in range(8):
        b, half = c // 2, c % 2
        xl.append(x[b] if half == 0 else x[b, ::-1])
    if FUSED:
        nc = _get_nc("fused", [(0, 4096, 3072), (1, 3072, 2048)], 4096, 2048)
        in_maps = [_core_inputs(c, xl[c], *rest) for c in range(8)]
        res = run_bass_kernel_spmd(nc, in_maps, core_ids=list(range(8)))
        outs = [r["out"] for r in res.results]
    else:
        nc1 = _get_nc("l0", [(0, 4096, 3072)], 4096, 3072)
        in_maps = [_core_inputs(c, xl[c], *rest) for c in range(8)]
        res = run_bass_kernel_spmd(nc1, in_maps, core_ids=list(range(8)))
        x1 = [r["out"] for r in res.results]
        nc2 = _get_nc("l1", [(1, 3072, 2048)], 3072, 2048)
        in_maps = [_core_inputs(c, x1[c], *rest) for c in range(8)]
        res = run_bass_kernel_spmd(nc2, in_maps, core_ids=list(range(8)))
        outs = [r["out"] for r in res.results]
    out = np.empty((B, SEQ, D), dtype=np.float32)
    for c in range(8):
        b, half = c // 2, c % 2
        if half == 0:
            out[b, 0:2048] = outs[c]
        else:
            out[b, 2048:4096] = outs[c][::-1]
    return out
```

```python
import numpy as np
import concourse.bass as bass
import concourse.mybir as mybir
from concourse.bass_utils import run_bass_kernel_spmd

F32 = mybir.dt.float32
BF16 = mybir.dt.bfloat16
AF = mybir.ActivationFunctionType
ALU = mybir.AluOpType

D = 1024
NIN = 12288
DA = 2048
SEQ = 4096
EPS = 1e-6
MW = 2560
SCALE = 128 ** -0.5
C_U, C_V, C_ZA, C_Q, C_K, C_VB, C_ZB, C_GA, C_GB = 0, 2048, 4096, 6144, 7168, 8192, 9216, 10240, 11264

ENGS = ("pe", "act", "dve", "pool", "sp")
SEM_LIMIT = 30000


class Sig:
    __slots__ = ("sem", "val", "eng")

    def __init__(self, sem, val, eng):
        self.sem, self.val, self.eng = sem, val, eng


class Tile:
    __slots__ = ("w", "rs")

    def __init__(self):
        self.w = None
        self.rs = {}


class Sched:
    def __init__(self, nc, sem_handles):
        self.nc = nc
        self.pool_sems = list(sem_handles)
        self.sems = []
        self.eng = {"pe": nc.tensor, "act": nc.scalar, "dve": nc.vector, "pool": nc.gpsimd, "sp": nc.sync}
        self.seen = {e: {} for e in ENGS}
        self.cur = {e: [self._alloc(), 0] for e in ("pe", "act", "dve", "pool")}
        self.rings = {"sp": [self._alloc() for _ in range(40)], "pool": [self._alloc() for _ in range(24)]}
        self.ring_i = {"sp": 0, "pool": 0}
        self.last = {}

    def _alloc(self):
        h = self.pool_sems.pop()
        self.sems.append(h)
        return len(self.sems) - 1

    def _emit_waits(self, eng, deps):
        best = {}
        for sg in deps:
            if sg is None:
                continue
            if sg.eng == "pe" and eng == "pe":
                continue
            if best.get(sg.sem, 0) < sg.val:
                best[sg.sem] = sg.val
        seen = self.seen[eng]
        for s, v in best.items():
            if seen.get(s, 0) >= v:
                continue
            seen[s] = v
            self.eng[eng].wait_ge(self.sems[s], v)

    def _deps(self, reads, writes):
        deps = []
        for t in reads:
            deps.append(t.w)
        for t in writes:
            deps.append(t.w)
            deps.extend(t.rs.values())
        return deps

    def op(self, eng, fns, reads=(), writes=()):
        if not isinstance(fns, (list, tuple)):
            fns = [fns]
        self._emit_waits(eng, self._deps(reads, writes))
        c = self.cur[eng]
        if c[1] >= SEM_LIMIT:
            c[0] = self._alloc()
            c[1] = 0
        c[1] += 1
        sig = Sig(c[0], c[1], eng)
        h = self.sems[c[0]]
        e = self.eng[eng]
        for f in fns[:-1]:
            f(e)
        fns[-1](e).then_inc(h, 1)
        for t in reads:
            t.rs[eng] = sig
        for t in writes:
            t.w = sig
            t.rs = {}
        self.last[eng] = sig
        return sig

    def dma(self, q, out_ap, in_ap, reads=(), writes=()):
        deps = self._deps(reads, writes)
        ring = self.rings[q]
        i = self.ring_i[q]
        self.ring_i[q] += 1
        s = ring[i % len(ring)]
        k = i // len(ring)
        if k > 0:
            deps.append(Sig(s, 16 * k, "dma"))
        self._emit_waits(q, deps)
        sig = Sig(s, 16 * (k + 1), "dma")
        h = self.sems[s]
        self.eng[q].dma_start(out=out_ap, in_=in_ap).then_inc(h, 16)
        for t in reads:
            t.rs[("d", s)] = sig
        for t in writes:
            t.w = sig
            t.rs = {}
        return sig

    def barrier(self):
        sigs = list(self.last.values())
        for q, ring in self.rings.items():
            n = self.ring_i[q]
            for idx, s in enumerate(ring):
                uses = (n - idx + len(ring) - 1) // len(ring) if n > idx else 0
                if uses > 0:
                    sigs.append(Sig(s, 16 * uses, "dma"))
        for e in ENGS:
            self._emit_waits(e, sigs)


def _build(layer_specs, n_x_rows, n_out_rows, debug=False):
    nc = bass.Bass("TRN2", target_bir_lowering=False)
    dt = nc.dram_tensor
    x_in = dt("x", [n_x_rows, D], F32, kind="ExternalInput").ap()
    w_in = dt("w_in", [2, D, NIN], F32, kind="ExternalInput").ap()
    w_pa = dt("w_pa", [2, DA, D], F32, kind="ExternalInput").ap()
    w_pb = dt("w_pb", [2, D, D], F32, kind="ExternalInput").ap()
    w_o = dt("w_o", [2, D, D], F32, kind="ExternalInput").ap()
    wsT_d = dt("wsT", [2, 16, 128, 128], F32, kind="ExternalInput").ap()
    bs_d = dt("bs", [2, 2048], F32, kind="ExternalInput").ap()
    gpre_d = dt("g_pre", [2, D], F32, kind="ExternalInput").ap()
    gpost_d = dt("g_post", [2, D], F32, kind="ExternalInput").ap()
    lng_d = dt("ln_g_t", [2, 128, 16], F32, kind="ExternalInput").ap()
    lnb_d = dt("ln_b_t", [2, 128, 16], F32, kind="ExternalInput").ap()
    bgt_d = dt("b_gate_t", [2, 128, 16], F32, kind="ExternalInput").ap()
    mask_d = dt("masks", [8, 128, MW], F32, kind="ExternalInput").ap()
    ident_d = dt("ident", [128, 128], F32, kind="ExternalInput").ap()
    out_d = dt("out", [n_out_rows, D], F32, kind="ExternalOutput").ap()

    skind = "ExternalOutput" if debug else "Internal"
    scr = []
    for li, (l, ntW, ntF) in enumerate(layer_specs):
        s = {}
        s["gv"] = dt(f"gv{li}", [ntF, DA], BF16, kind=skind).ap()
        s["uzT"] = dt(f"uzT{li}", [16, 128, ntF], BF16, kind=skind).ap()
        s["qT"] = dt(f"qT{li}", [8, 128, ntF], BF16, kind=skind).ap()
        s["kT"] = dt(f"kT{li}", [8, 128, ntW], BF16, kind=skind).ap()
        s["v1"] = dt(f"v1{li}", [8, 128, ntW // 128, 129], BF16, kind=skind).ap()
        s["zbT"] = dt(f"zbT{li}", [8, 128, ntF], BF16, kind=skind).ap()
        s["gT"] = dt(f"gT{li}", [16, 128, ntF], BF16, kind=skind).ap()
        s["ybT"] = dt(f"ybT{li}", [8, 128, ntF], BF16, kind=skind).ap()
        if li < len(layer_specs) - 1:
            s["xo"] = dt(f"xmid{li}", [ntF, D], F32, kind=skind).ap()
        else:
            s["xo"] = out_d
        scr.append(s)

    from contextlib import ExitStack
    with ExitStack() as top:
        sem_handles = [top.enter_context(nc.semaphore(f"s{i}")) for i in range(84)]
        S = Sched(nc, sem_handles)

        uid = [0]

        def sb(stack, name, shape, dtype):
            uid[0] += 1
            return stack.enter_context(nc.sbuf_tensor(f"sb{uid[0]}_{name}", shape, dtype))

        def ps(stack, name, shape, dtype):
            uid[0] += 1
            return stack.enter_context(nc.psum_tensor(f"ps{uid[0]}_{name}", shape, dtype))

        ident_f = sb(top, "ident_f", [128, 128], F32)
        ident = sb(top, "ident", [128, 128], BF16)
        ones_bf = sb(top, "ones_bf", [128, 128], BF16)
        T_ident_f, T_ident, T_ones = Tile(), Tile(), Tile()
        S.dma("sp", ident_f[:], ident_d[:, :], writes=[T_ident_f])
        S.op("pool", lambda e: e.tensor_copy(out=ident[:], in_=ident_f[:]), reads=[T_ident_f], writes=[T_ident])
        S.op("pool", lambda e: e.memset(ones_bf[:], 1.0), writes=[T_ones])

        x_src = x_in
        for li, (l, ntW, ntF) in enumerate(layer_specs):
            sc = scr[li]
            nTW, nTF = ntW // 128, ntF // 128
            nBF = ntF // 512
            nBW = ntW // 512
            dT = {}

            def dtile(*key):
                if key not in dT:
                    dT[key] = Tile()
                return dT[key]

            with ExitStack() as st:
                hT = sb(st, "hT", [128, 8, ntW], BF16)
                hT_t = [Tile() for _ in range(nTW)]
                gpre_bc = sb(st, "gpre_bc", [128, D], F32)
                T_gpre = Tile()
                S.dma("sp", gpre_bc[:], gpre_d[l].partition_broadcast(128), writes=[T_gpre])
                bgt = sb(st, "bgt", [128, 16], F32)
                T_bgt = Tile()
                S.dma("sp", bgt[:], bgt_d[l], writes=[T_bgt])
                if True:
                    s1 = st
                    wst = [sb(s1, f"wst{i}", [128, 8, 512], F32) for i in range(2)]
                    wb = [sb(s1, f"wb{i}", [128, 8, 512], BF16) for i in range(3)]
                    T_wst = [Tile() for _ in range(2)]
                    T_wb = [Tile() for _ in range(3)]
                    ust = sb(s1, "ust", [128, 4, ntF], BF16)
                    T_ust = {}
                    ost = [sb(s1, f"ost{i}", [128, 512], BF16) for i in range(4)]
                    T_ost = [Tile() for _ in range(4)]
                    zt = [sb(s1, f"zt{i}", [128, 512], BF16) for i in range(2)]
                    T_zt = [Tile() for _ in range(2)]
                    v1st = [sb(s1, f"v1st{i}", [128, 4, 129], BF16) for i in range(2)]
                    T_v1st = [Tile() for _ in range(2)]
                    for i in range(2):
                        S.op("pool", lambda e, i=i: e.memset(v1st[i][:], 1.0), writes=[T_v1st[i]])
                    acc = [ps(s1, f"acc{i}", [128, 512], F32) for i in range(4)]
                    T_acc = [Tile() for _ in range(4)]
                    cnt = {"wst": 0, "wb": 0, "acc": 0, "ost": 0, "zt": 0, "v1": 0}

                    T_wst_h = [[Tile(), Tile()] for _ in range(2)]
                    blk_order = ([C_K, C_K + 512, C_VB, C_VB + 512, C_Q, C_Q + 512, C_ZB, C_ZB + 512,
                                  C_GA, C_GA + 512, C_GB, C_GB + 512] + [C_V + 512 * i for i in range(4)])
                    for i in range(4):
                        blk_order += [C_U + 512 * i, C_ZA + 512 * i]
                    blk_slot = {}

                    def ensure_loaded(k):
                        if k >= len(blk_order) or k in blk_slot:
                            return
                        c0 = blk_order[k]
                        a = cnt["wst"] % 2
                        cnt["wst"] += 1
                        b = cnt["wb"] % 3
                        cnt["wb"] += 1
                        src = w_in[l, :, c0:c0 + 512].rearrange("(kc k) c -> k kc c", k=128)
                        S.dma("sp", wst[a][:, 0:4, :], src[:, 0:4, :], writes=[T_wst_h[a][0]])
                        S.dma("sp", wst[a][:, 4:8, :], src[:, 4:8, :], writes=[T_wst_h[a][1]])
                        S.op("dve", lambda e, a=a, b=b: e.tensor_copy(out=wb[b][:, 0:4, :], in_=wst[a][:, 0:4, :]),
                             reads=[T_wst_h[a][0]], writes=[T_wb[b]])
                        S.op("dve", lambda e, a=a, b=b: e.tensor_copy(out=wb[b][:, 4:8, :], in_=wst[a][:, 4:8, :]),
                             reads=[T_wst_h[a][1]], writes=[T_wb[b]])
                        blk_slot[k] = b

                    blk_next = [0]

                    def load_block2(c0):
                        k = blk_next[0]
                        blk_next[0] += 1
                        assert blk_order[k] == c0, (k, c0)
                        ensure_loaded(k)
                        ensure_loaded(k + 1)
                        return blk_slot[k]

                    def fm_mm(b, c, tb):
                        p = cnt["acc"] % 4
                        cnt["acc"] += 1
                        fns = [lambda e, p=p, b=b, c=c, tb=tb, kc=kc: e.matmul(
                            acc[p][:], lhsT=wb[b][:, kc, c * 128:(c + 1) * 128],
                            rhs=hT[:, kc, tb * 512:(tb + 1) * 512], start=(kc == 0), stop=(kc == 7)) for kc in range(8)]
                        S.op("pe", fns, reads=[T_wb[b]] + hT_t[tb * 4:tb * 4 + 4], writes=[T_acc[p]])
                        return p

                    def tm_mm(b, j):
                        p = cnt["acc"] % 4
                        cnt["acc"] += 1
                        fns = [lambda e, p=p, b=b, j=j, kc=kc: e.matmul(
                            acc[p][:], lhsT=hT[:, kc, j * 128:(j + 1) * 128],
                            rhs=wb[b][:, kc, :], start=(kc == 0), stop=(kc == 7)) for kc in range(8)]
                        S.op("pe", fns, reads=[T_wb[b], hT_t[j]], writes=[T_acc[p]])
                        return p

                    def fm_simple(c0, nblk_tok, func, dst, dkey, chunk0, bias_col0=None):
                        b = load_block2(c0)
                        for c in range(4):
                            for tb in range(nblk_tok):
                                p = fm_mm(b, c, tb)
                                o = cnt["ost"] % 4
                                cnt["ost"] += 1
                                if bias_col0 is None:
                                    S.op("act", lambda e, o=o, p=p: e.activation(out=ost[o][:], in_=acc[p][:], func=func),
                                         reads=[T_acc[p]], writes=[T_ost[o]])
                                else:
                                    bc = bias_col0 + c
                                    S.op("act", lambda e, o=o, p=p, bc=bc: e.activation(
                                        out=ost[o][:], in_=acc[p][:], func=func, bias=bgt[:, bc:bc + 1]),
                                         reads=[T_acc[p], T_bgt], writes=[T_ost[o]])
                                S.dma("pool", dst[chunk0 + c][:, tb * 512:(tb + 1) * 512], ost[o][:],
                                      reads=[T_ost[o]], writes=[dtile(dkey, chunk0 + c, tb)])

                    ensure_loaded(0)
                    ensure_loaded(1)
                if True:
                    s0 = st
                    NX = 6
                    xt = [sb(s0, f"xt{i}", [128, D], F32) for i in range(NX)]
                    xn = [sb(s0, f"xn{i}", [128, D], BF16) for i in range(4)]
                    junk = sb(s0, "junk0", [128, D], F32)
                    ssb = [sb(s0, f"ss{i}", [128, 4], F32) for i in range(2)]
                    pT = [ps(s0, f"pT{i}", [128, 8, 128], BF16) for i in range(4)]
                    T_xt = [Tile() for _ in range(NX)]
                    T_xn = [Tile() for _ in range(4)]
                    T_junk = Tile()
                    T_ss = [Tile() for _ in range(2)]
                    T_pT = [Tile() for _ in range(4)]
                    ssall = sb(s0, "ssall", [128, nTW], F32)
                    rstdall = sb(s0, "rstdall", [128, nTW], F32)
                    T_ssA = [Tile() for _ in range(nBW)]
                    T_rstdA = [Tile() for _ in range(nBW)]
                    xcnt = [0]

                    def phaseA(tb):
                        for j in range(tb * 4, tb * 4 + 4):
                            a = xcnt[0] % NX
                            xcnt[0] += 1
                            S.dma("sp", xt[a][:], x_src[j * 128:(j + 1) * 128, :], writes=[T_xt[a]])
                            S.op("act", lambda e, a=a, j=j: e.activation(out=junk[:], in_=xt[a][:], func=AF.Square,
                                                                         accum_out=ssall[:, j:j + 1]),
                                 reads=[T_xt[a]], writes=[T_junk, T_ssA[tb]])
                        S.op("act", lambda e, tb=tb: e.activation(out=rstdall[:, tb * 4:tb * 4 + 4], in_=ssall[:, tb * 4:tb * 4 + 4],
                                                                  func=AF.Sqrt, bias=EPS, scale=1.0 / D),
                             reads=[T_ssA[tb]], writes=[T_rstdA[tb]])
                        S.op("dve", lambda e, tb=tb: e.reciprocal(out=rstdall[:, tb * 4:tb * 4 + 4], in_=rstdall[:, tb * 4:tb * 4 + 4]),
                             writes=[T_rstdA[tb]])

                    phaseA(0)
                    if nBW > 1:
                        phaseA(1)

                    def stage0_norm(j):
                        a = xcnt[0] % NX
                        xcnt[0] += 1
                        b = j % 4
                        S.dma("sp", xt[a][:], x_src[j * 128:(j + 1) * 128, :], writes=[T_xt[a]])
                        S.op("dve", lambda e, a=a, b=b, j=j: e.scalar_tensor_tensor(
                            out=xn[b][:], in0=xt[a][:], scalar=rstdall[:, j:j + 1], in1=gpre_bc[:],
                            op0=ALU.mult, op1=ALU.mult),
                             reads=[T_xt[a], T_rstdA[j // 4], T_gpre], writes=[T_xn[b]])

                    def stage0_tr(j):
                        b = j % 4
                        fns = [lambda e, b=b, kc=kc: e.transpose(out=pT[b][:, kc, :], in_=xn[b][:, kc * 128:(kc + 1) * 128],
                                                                 identity=ident[:]) for kc in range(8)]
                        S.op("pe", fns, reads=[T_xn[b], T_ident], writes=[T_pT[b]])

                    def stage0_cp(j):
                        b = j % 4
                        S.op("dve", lambda e, b=b, j=j: e.tensor_copy(out=hT[:, :, j * 128:(j + 1) * 128], in_=pT[b][:]),
                             reads=[T_pT[b]], writes=[hT_t[j]])

                    kb = [load_block2(C_K), load_block2(C_K + 512)]
                    for fn_ in (stage0_norm, stage0_tr, stage0_cp):
                        for j in range(0, 4):
                            fn_(j)
                    for tb in range(nBW):
                        if tb + 2 < nBW:
                            phaseA(tb + 2)
                        nxt_j = range(tb * 4 + 4, tb * 4 + 8) if tb + 1 < nBW else ()
                        for j in nxt_j:
                            stage0_norm(j)
                        for bb in range(2):
                            if bb == 1:
                                for j in nxt_j:
                                    stage0_tr(j)
                                for j in nxt_j:
                                    stage0_cp(j)
                            for c in range(4):
                                p = fm_mm(kb[bb], c, tb)
                                o = cnt["ost"] % 4
                                cnt["ost"] += 1
                                S.op("act", lambda e, o=o, p=p: e.activation(out=ost[o][:], in_=acc[p][:], func=AF.Copy),
                                     reads=[T_acc[p]], writes=[T_ost[o]])
                                S.dma("pool", sc["kT"][bb * 4 + c][:, tb * 512:(tb + 1) * 512], ost[o][:],
                                      reads=[T_ost[o]], writes=[dtile("kT", bb * 4 + c, tb)])
                    for bb in range(2):
                        b = load_block2(C_VB + bb * 512)
                        for j in range(nTW):
                            p = tm_mm(b, j)
                            o = cnt["v1"] % 2
                            cnt["v1"] += 1
                            S.op("act", lambda e, o=o, p=p: e.activation(
                                out=v1st[o][:, :, 0:128], in_=acc[p][:].rearrange("p (h e) -> p h e", e=128), func=AF.Copy),
                                 reads=[T_acc[p]], writes=[T_v1st[o]])
                            S.dma("pool", sc["v1"][bb * 4:(bb + 1) * 4, :, j, :].rearrange("h p e -> p h e"), v1st[o][:],
                                  reads=[T_v1st[o]], writes=[dtile("v1", bb, j)])
                    for bb in range(2):
                        fm_simple(C_Q + bb * 512, nBF, AF.Copy, sc["qT"], "qT", bb * 4)
                    for bb in range(2):
                        fm_simple(C_ZB + bb * 512, nBF, AF.Silu, sc["zbT"], "zbT", bb * 4)
                    for bb in range(2):
                        fm_simple(C_GA + bb * 512, nBF, AF.Sigmoid, sc["gT"], "gT", bb * 4, bias_col0=bb * 4)
                    for bb in range(2):
                        fm_simple(C_GB + bb * 512, nBF, AF.Sigmoid, sc["gT"], "gT", 8 + bb * 4, bias_col0=8 + bb * 4)
                    for bb in range(4):
                        b = load_block2(C_V + bb * 512)
                        for j in range(nTF):
                            p = tm_mm(b, j)
                            o = cnt["ost"] % 4
                            cnt["ost"] += 1
                            S.op("act", lambda e, o=o, p=p: e.activation(out=ost[o][:], in_=acc[p][:], func=AF.Gelu_apprx_tanh),
                                 reads=[T_acc[p]], writes=[T_ost[o]])
                            S.dma("pool", sc["gv"][j * 128:(j + 1) * 128, bb * 512:(bb + 1) * 512], ost[o][:],
                                  reads=[T_ost[o]], writes=[dtile("gv", bb, j)])
                    for bb in range(4):
                        b = load_block2(C_U + bb * 512)
                        for c in range(4):
                            for tb in range(nBF):
                                p = fm_mm(b, c, tb)
                                tk = (c, tb)
                                if tk not in T_ust:
                                    T_ust[tk] = Tile()
                                S.op("act", lambda e, c=c, tb=tb, p=p: e.activation(
                                    out=ust[:, c, tb * 512:(tb + 1) * 512], in_=acc[p][:], func=AF.Gelu_apprx_tanh),
                                     reads=[T_acc[p]], writes=[T_ust[tk]])
                        b = load_block2(C_ZA + bb * 512)
                        for c in range(4):
                            for tb in range(nBF):
                                p = fm_mm(b, c, tb)
                                z = cnt["zt"] % 2
                                cnt["zt"] += 1
                                S.op("act", lambda e, z=z, p=p: e.activation(out=zt[z][:], in_=acc[p][:], func=AF.Silu),
                                     reads=[T_acc[p]], writes=[T_zt[z]])
                                o = cnt["ost"] % 4
                                cnt["ost"] += 1
                                S.op("dve", lambda e, o=o, z=z, c=c, tb=tb: e.tensor_tensor(
                                    out=ost[o][:], in0=zt[z][:], in1=ust[:, c, tb * 512:(tb + 1) * 512], op=ALU.mult),
                                     reads=[T_zt[z], T_ust[(c, tb)]], writes=[T_ost[o]])
                                S.dma("pool", sc["uzT"][bb * 4 + c][:, tb * 512:(tb + 1) * 512], ost[o][:],
                                      reads=[T_ost[o]], writes=[dtile("uzT", bb * 4 + c, tb)])
                S.barrier()

            with ExitStack() as st:
                wpa = sb(st, "wpa", [128, 16, D], BF16)
                wpb = sb(st, "wpb", [128, 8, D], BF16)
                wo = sb(st, "wo", [128, 8, D], BF16)
                T_wpa = [Tile() for _ in range(16)]
                T_wpb = [Tile() for _ in range(8)]
                T_wo = [Tile() for _ in range(8)]
                wsT_bf = sb(st, "wsT_bf", [128, 16, 128], BF16)
                T_wsT = Tile()
                biasT = sb(st, "biasT", [128, 16, 2, 128], F32)
                T_biasT = Tile()
                lng = sb(st, "lng", [128, 16], F32)
                lnb = sb(st, "lnb", [128, 16], F32)
                T_lng, T_lnb = Tile(), Tile()
                gpost_bc = sb(st, "gpost_bc", [128, D], F32)
                T_gpost = Tile()
                S.dma("sp", lng[:], lng_d[l], writes=[T_lng])
                S.dma("sp", lnb[:], lnb_d[l], writes=[T_lnb])
                S.dma("sp", gpost_bc[:], gpost_d[l].partition_broadcast(128), writes=[T_gpost])

                with ExitStack() as s2:
                    wst2 = [sb(s2, f"wst2_{i}", [128, 2, D], F32) for i in range(2)]
                    T_wst2 = [Tile() for _ in range(2)]
                    wcnt = [0]

                    def load_proj(src_rows_ap, dst, dst_tiles, kc0, nkc):
                        a = wcnt[0] % 2
                        wcnt[0] += 1
                        S.dma("sp", wst2[a][:, 0:nkc, :], src_rows_ap.rearrange("(kc k) c -> k kc c", k=128),
                              writes=[T_wst2[a]])
                        S.op("pool", lambda e, a=a: e.tensor_copy(out=dst[:, kc0:kc0 + nkc, :], in_=wst2[a][:, 0:nkc, :]),
                             reads=[T_wst2[a]], writes=dst_tiles[kc0:kc0 + nkc])

                    proj_jobs = []
                    for q4 in range(8):
                        proj_jobs.append((w_pa[l, q4 * 256:(q4 + 1) * 256, :], wpa, T_wpa, q4 * 2, 2))
                    for q4 in range(4):
                        proj_jobs.append((w_pb[l, q4 * 256:(q4 + 1) * 256, :], wpb, T_wpb, q4 * 2, 2))
                    for q4 in range(4):
                        proj_jobs.append((w_o[l, q4 * 256:(q4 + 1) * 256, :], wo, T_wo, q4 * 2, 2))

                    wsT_f = wst2[0][:].rearrange("p a (g q) -> p (a g) q", q=128)
                    bs_bc = wst2[1][:].rearrange("p a b -> p (a b)")
                    T_wsTf, T_bsbc = T_wst2[0], T_wst2[1]
                    S.dma("sp", wsT_f, wsT_d[l].rearrange("g q p -> q g p"), writes=[T_wsTf])
                    S.dma("sp", bs_bc, bs_d[l].partition_broadcast(128), writes=[T_bsbc])
                    S.op("pool", lambda e: e.tensor_copy(out=wsT_bf[:], in_=wsT_f), reads=[T_wsTf], writes=[T_wsT])

                    kTh = [sb(s2, f"kTh{i}", [128, ntW], BF16) for i in range(2)]
                    v1h = [sb(s2, f"v1h{i}", [128, nTW, 129], BF16) for i in range(2)]
                    qTh = [sb(s2, f"qTh{i}", [128, ntF], BF16) for i in range(2)]
                    zbh = [sb(s2, f"zbh{i}", [128, ntF], BF16) for i in range(2)]
                    mkf = [sb(s2, "mkf0", [128, MW], F32)]
                    mk = [sb(s2, f"mk{i}", [128, MW], BF16) for i in range(2)]
                    T_kTh = [Tile() for _ in range(2)]
                    T_v1h = [Tile() for _ in range(2)]
                    T_qTh = [Tile() for _ in range(2)]
                    T_zbh = [Tile() for _ in range(2)]
                    T_mkf = [Tile()]
                    T_mk = [Tile() for _ in range(2)]
                    LOOK = 3
                    NS = LOOK + 1
                    NE = LOOK + 2
                    et = [sb(s2, f"et{i}", [128, 512], BF16) for i in range(NE)]
                    pt = [sb(s2, f"pt{i}", [128, 512], BF16) for i in range(NE)]
                    T_et = [Tile() for _ in range(NE)]
                    T_pt = [Tile() for _ in range(NE)]
                    on_t = [sb(s2, f"on{i}", [128, 128], BF16) for i in range(2)]
                    T_on = [Tile() for _ in range(2)]
                    rden = [sb(s2, f"rden{i}", [128, 2], F32) for i in range(2)]
                    T_rden = [Tile() for _ in range(2)]
                    ybst = [sb(s2, f"ybst{i}", [128, 512], BF16) for i in range(2)]
                    T_ybst = [Tile() for _ in range(2)]
                    ybc = [sb(s2, f"ybc{i}", [128, 512], BF16) for i in range(2)]
                    T_ybc = [Tile() for _ in range(2)]
                    stp = [ps(s2, f"stp{i}", [128, 512], F32) for i in range(NS)]
                    ops_ = [ps(s2, f"ops{i}", [128, 512], F32) for i in range(2)]
                    tpp = [ps(s2, f"tpp{i}", [128, 1024], BF16) for i in range(2)]
                    T_stp = [Tile() for _ in range(NS)]
                    T_ops = [Tile() for _ in range(2)]
                    T_tpp = [Tile() for _ in range(2)]

                    for g4 in range(4):
                        pslot = g4 % 2
                        fns = [lambda e, pslot=pslot, g=g4 * 4 + gi, gi=gi: e.matmul(
                            stp[pslot][:, gi * 128:(gi + 1) * 128], lhsT=ones_bf[:], rhs=wsT_bf[:, g, :],
                            start=True, stop=True) for gi in range(4)]
                        S.op("pe", fns, reads=[T_ones, T_wsT], writes=[T_stp[pslot]])
                        for gi in range(4):
                            g = g4 * 4 + gi
                            for cdup in range(2):
                                S.op("dve", lambda e, pslot=pslot, g=g, gi=gi, cdup=cdup: e.scalar_tensor_tensor(
                                    out=biasT[:, g, cdup, :], in0=stp[pslot][:, gi * 128:(gi + 1) * 128], scalar=lnb[:, g:g + 1],
                                    in1=bs_bc[:, g * 128:(g + 1) * 128], op0=ALU.mult, op1=ALU.add),
                                     reads=[T_stp[pslot], T_lnb, T_bsbc], writes=[T_biasT])

                    def load_head(h):
                        s_ = h % 2
                        S.dma("sp", kTh[s_][:], sc["kT"][h], reads=[dtile("kT", h, tb) for tb in range(nBW)], writes=[T_kTh[s_]])
                        S.dma("sp", v1h[s_][:], sc["v1"][h], reads=[dtile("v1", h // 4, j) for j in range(nTW)], writes=[T_v1h[s_]])
                        S.dma("sp", qTh[s_][:], sc["qT"][h], reads=[dtile("qT", h, tb) for tb in range(nBF)], writes=[T_qTh[s_]])
                        S.dma("sp", zbh[s_][:], sc["zbT"][h], reads=[dtile("zbT", h, tb) for tb in range(nBF)], writes=[T_zbh[s_]])
                        S.dma("sp", mkf[0][:], mask_d[h], writes=[T_mkf[0]])
                        S.op("pool", lambda e, s_=s_: e.tensor_copy(out=mk[s_][:], in_=mkf[0][:]), reads=[T_mkf[0]], writes=[T_mk[s_]])

                    load_head(0)
                    tix = [0]
                    for h in range(8):
                        hs = h % 2
                        if h + 1 < 8:
                            load_head(h + 1)
                        for _ in range(2):
                            if proj_jobs:
                                load_proj(*proj_jobs.pop(0))
                        R = HEAD_RADIUS[h]
                        groups = []
                        for i in range(nTF):
                            j_hi = min(nTW - 1, i + R)
                            j_lo = max(0, i - R)
                            js = list(range(j_hi, j_lo - 1, -1))
                            gl = [js[a:a + 4] for a in range(0, len(js), 4)]
                            for gi, g in enumerate(gl):
                                groups.append((i, g, gi == 0, gi == len(gl) - 1, tix[0]))
                            tix[0] += 1
                        G = len(groups)

                        def emit_qk(idx):
                            i, js, first, last, tx = groups[idx]
                            n = len(js)
                            ss_, es = idx % NS, idx % NE
                            fns = [lambda e, ss_=ss_, m=m, j=j, i=i: e.matmul(
                                stp[ss_][:, m * 128:(m + 1) * 128], lhsT=kTh[hs][:, j * 128:(j + 1) * 128],
                                rhs=qTh[hs][:, i * 128:(i + 1) * 128], start=True, stop=True) for m, j in enumerate(js)]
                            S.op("pe", fns, reads=[T_kTh[hs], T_qTh[hs]], writes=[T_stp[ss_]])
                            S.op("act", lambda e, ss_=ss_, es=es, n=n: e.activation(
                                out=et[es][:, 0:n * 128], in_=stp[ss_][:, 0:n * 128], func=AF.Exp, scale=SCALE),
                                 reads=[T_stp[ss_]], writes=[T_et[es]])
                            x0 = 1024 - (js[0] - i) * 128
                            S.op("dve", lambda e, es=es, n=n, x0=x0: e.tensor_tensor(
                                out=pt[es][:, 0:n * 128], in0=et[es][:, 0:n * 128], in1=mk[hs][:, x0:x0 + n * 128], op=ALU.mult),
                                 reads=[T_et[es], T_mk[hs]], writes=[T_pt[es]])

                        def emit_pv(idx):
                            i, js, first, last, tx = groups[idx]
                            es = idx % NE
                            osl = tx % 2
                            n = len(js)
                            fns = [lambda e, osl=osl, es=es, m=m, j=j, first=first, last=last, n=n: e.matmul(
                                ops_[osl][:, 0:129], lhsT=pt[es][:, m * 128:(m + 1) * 128], rhs=v1h[hs][:, j, :],
                                start=(first and m == 0), stop=(last and m == n - 1)) for m, j in enumerate(js)]
                            S.op("pe", fns, reads=[T_pt[es], T_v1h[hs]], writes=[T_ops[osl]])

                        def emit_fin_dve(idx):
                            i, js, first, last, tx = groups[idx]
                            osl = tx % 2
                            S.op("dve", lambda e, osl=osl: e.reciprocal(out=rden[osl][:, 0:1], in_=ops_[osl][:, 128:129]),
                                 reads=[T_ops[osl]], writes=[T_rden[osl]])

                        def emit_fin_act(idx):
                            i, js, first, last, tx = groups[idx]
                            osl = tx % 2
                            S.op("act", lambda e, osl=osl: e.activation(
                                out=on_t[osl][:], in_=ops_[osl][:, 0:128], func=AF.Identity, scale=rden[osl][:, 0:1]),
                                 reads=[T_ops[osl], T_rden[osl]], writes=[T_on[osl]])

                        def emit_fin_pe(idx):
                            i, js, first, last, tx = groups[idx]
                            osl = tx % 2
                            c = i % 4
                            tb = i // 4
                            tsl = tb % 2
                            S.op("pe", lambda e, osl=osl, tsl=tsl, c=c: e.transpose(
                                out=tpp[tsl][:, c * 128:(c + 1) * 128], in_=on_t[osl][:], identity=ident[:]),
                                 reads=[T_on[osl], T_ident], writes=[T_tpp[tsl]])

                        def emit_fin_yb(idx):
                            i, js, first, last, tx = groups[idx]
                            c = i % 4
                            tb = i // 4
                            tsl = tb % 2
                            if c == 3:
                                S.op("dve", lambda e, tsl=tsl: e.tensor_copy(out=ybc[tsl][:], in_=tpp[tsl][:, 0:512]),
                                     reads=[T_tpp[tsl]], writes=[T_ybc[tsl]])
                                S.op("pool", lambda e, tsl=tsl, tb=tb: e.tensor_tensor(
                                    out=ybst[tsl][:], in0=ybc[tsl][:], in1=zbh[hs][:, tb * 512:(tb + 1) * 512],
                                    op=ALU.mult),
                                     reads=[T_ybc[tsl], T_zbh[hs]], writes=[T_ybst[tsl]])
                                S.dma("pool", sc["ybT"][h][:, tb * 512:(tb + 1) * 512], ybst[tsl][:],
                                      reads=[T_ybst[tsl]], writes=[dtile("ybT", h, tb)])

                        for idx in range(G + LOOK + 6):
                            for dk, fn in ((1, emit_fin_dve), (2, emit_fin_act), (3, emit_fin_pe), (5, emit_fin_yb)):
                                k = idx - LOOK - dk
                                if 0 <= k < G and groups[k][3]:
                                    fn(k)
                            if idx < G:
                                emit_qk(idx)
                            k = idx - LOOK
                            if 0 <= k < G:
                                emit_pv(k)
                    while proj_jobs:
                        load_proj(*proj_jobs.pop(0))
                S.barrier()

                with ExitStack() as s3:
                    TB = 256
                    gvt = [sb(s3, "gvt0", [128, 2, DA], BF16)]
                    uzt = [sb(s3, f"uzt{i}", [128, 16, TB], BF16) for i in range(2)]
                    gt_ = [sb(s3, f"gtt{i}", [128, 16, TB], BF16) for i in range(2)]
                    ybt = [sb(s3, f"ybt{i}", [128, 8, TB], BF16) for i in range(2)]
                    xrt = [sb(s3, f"xrt{i}", [128, 2, D], F32) for i in range(2)]
                    T_gvt = [Tile()]
                    T_uzt = [[Tile() for _ in range(16)] for _ in range(2)]
                    T_gt = [Tile() for _ in range(2)]
                    T_ybt = [Tile() for _ in range(2)]
                    T_xrt = [Tile() for _ in range(2)]
                    vhat = [sb(s3, f"vhat{i}", [128, 2, DA], BF16) for i in range(2)]
                    T_vhat = [[Tile(), Tile()] for _ in range(2)]
                    lnst = sb(s3, "lnst", [128, 2, 8], F32)
                    T_lnst = [Tile(), Tile()]
                    junkb = sb(s3, "junkb", [128, DA], BF16)
                    T_junkb = Tile()
                    neghalf = sb(s3, "neghalf", [128, 2], F32)
                    T_neghalf = Tile()
                    S.op("pool", lambda e: e.memset(neghalf[:], -0.5), writes=[T_neghalf])
                    tmp = [sb(s3, f"tmp{i}", [128, TB], F32) for i in range(2)]
                    T_tmp = [Tile() for _ in range(2)]
                    t1 = [sb(s3, f"t1_{i}", [128, TB], F32) for i in range(2)]
                    T_t1 = [Tile() for _ in range(2)]
                    t2all = sb(s3, "t2all", [128, 8, TB], F32)
                    T_t2 = [Tile() for _ in range(8)]
                    mT = sb(s3, "mT", [128, 8, TB], BF16)
                    T_mT = [Tile() for _ in range(8)]
                    junk3 = sb(s3, "junk3", [128, 512], F32)
                    T_junk3 = Tile()
                    ssr = [sb(s3, f"ssr{i}", [128, 4], F32) for i in range(2)]
                    T_ssr = [Tile() for _ in range(2)]
                    yt = [sb(s3, f"yt{i}", [128, D], F32) for i in range(2)]
                    T_yt = [Tile() for _ in range(2)]
                    ot = [sb(s3, f"ot{i}", [128, D], F32) for i in range(2)]
                    T_ot = [Tile() for _ in range(2)]
                    spp = [ps(s3, f"spp{i}", [128, 512], F32) for i in range(2)]
                    pbp = [ps(s3, "pbp0", [128, 512], F32)]
                    pap = [ps(s3, f"pap{i}", [128, 512], F32) for i in range(2)]
                    rp = [ps(s3, f"rp{i}", [128, 512], F32) for i in range(3)]
                    T_spp = [Tile() for _ in range(2)]
                    T_pap = [Tile() for _ in range(2)]
                    T_pbp = [Tile()]
                    T_rp = [Tile() for _ in range(3)]
                    nblk = ntF // TB
                    c3 = {"sp": 0, "pa": 0, "pb": 0, "y": 0, "rp": 0}

                    def load3a(bi):
                        t0 = bi * TB
                        S.dma("sp", gvt[0][:], sc["gv"][t0:t0 + TB, :].rearrange("(c p) f -> p c f", p=128),
                              reads=[dtile("gv", bb, t0 // 128 + cc) for bb in range(4) for cc in range(2)], writes=[T_gvt[0]])

                    def load3b(bi):
                        s_ = bi % 2
                        t0 = bi * TB
                        tb512 = t0 // 512
                        S.dma("sp", uzt[s_][:], sc["uzT"][:, :, t0:t0 + TB].rearrange("g e t -> e g t"),
                              reads=[dtile("uzT", g, tb512) for g in range(16)], writes=T_uzt[s_])
                        S.dma("sp", ybt[s_][:], sc["ybT"][:, :, t0:t0 + TB].rearrange("g e t -> e g t"),
                              reads=[dtile("ybT", g, tb512) for g in range(8)], writes=[T_ybt[s_]])
                        S.dma("sp", gt_[s_][:], sc["gT"][:, :, t0:t0 + TB].rearrange("g e t -> e g t"),
                              reads=[dtile("gT", g, tb512) for g in range(16)], writes=[T_gt[s_]])
                        S.dma("sp", xrt[s_][:], x_src[t0:t0 + TB, :].rearrange("(c p) f -> p c f", p=128), writes=[T_xrt[s_]])

                    def ln_sums(bi):
                        for c in range(2):
                            S.op("act", lambda e, c=c: e.activation(out=junkb[:], in_=gvt[0][:, c, :], func=AF.Copy,
                                                                    accum_out=lnst[:, c, 0:1]),
                                 reads=[T_gvt[0]], writes=[T_junkb, T_lnst[c]])
                            S.op("act", lambda e, c=c: e.activation(out=junkb[:], in_=gvt[0][:, c, :], func=AF.Square,
                                                                    accum_out=lnst[:, c, 1:2]),
                                 reads=[T_gvt[0]], writes=[T_junkb, T_lnst[c]])

                    def ln_small(bi):
                        for c in range(2):
                            ops_l = [
                                lambda e, c=c: e.tensor_scalar(out=lnst[:, c, 2:3], in0=lnst[:, c, 0:1], scalar1=-1.0 / DA, scalar2=None, op0=ALU.mult),
                                lambda e, c=c: e.tensor_tensor(out=lnst[:, c, 3:4], in0=lnst[:, c, 2:3], in1=lnst[:, c, 2:3], op=ALU.mult),
                                lambda e, c=c: e.tensor_scalar(out=lnst[:, c, 4:5], in0=lnst[:, c, 1:2], scalar1=1.0 / DA, scalar2=EPS, op0=ALU.mult, op1=ALU.add),
                                lambda e, c=c: e.tensor_tensor(out=lnst[:, c, 5:6], in0=lnst[:, c, 4:5], in1=lnst[:, c, 3:4], op=ALU.subtract),
                                lambda e, c=c: e.tensor_tensor(out=lnst[:, c, 6:7], in0=lnst[:, c, 5:6], in1=neghalf[:, 0:1], op=ALU.pow),
                                lambda e, c=c: e.tensor_tensor(out=lnst[:, c, 7:8], in0=lnst[:, c, 2:3], in1=lnst[:, c, 6:7], op=ALU.mult),
                            ]
                            for f in ops_l:
                                S.op("pool", f, reads=[T_neghalf], writes=[T_lnst[c]])

                    def ln_vhat(bi):
                        vs = bi % 2
                        for c in range(2):
                            S.op("act", lambda e, c=c, vs=vs: e.activation(
                                out=vhat[vs][:, c, :], in_=gvt[0][:, c, :], func=AF.Identity,
                                scale=lnst[:, c, 6:7], bias=lnst[:, c, 7:8]),
                                 reads=[T_gvt[0], T_lnst[c]], writes=[T_vhat[vs][c]])

                    def phase_spatial_pb(bi):
                        s_, vs = bi % 2, bi % 2

                        def pool_mult(p, g):
                            S.op("pool", lambda e, p=p, g=g: e.tensor_tensor(
                                out=uzt[s_][:, g, :], in0=tmp[p][:], in1=uzt[s_][:, g, :], op=ALU.mult),
                                 reads=[T_tmp[p]], writes=[T_uzt[s_][g]])

                        def pb(jc):
                            pq = 0
                            fns = [lambda e, pq=pq, hh=hh, jc=jc: e.matmul(
                                pbp[0][:, 0:TB], lhsT=wpb[:, hh, jc * 128:(jc + 1) * 128], rhs=ybt[s_][:, hh, :],
                                start=(hh == 0), stop=(hh == 7)) for hh in range(8)]
                            S.op("pe", fns, reads=T_wpb + [T_ybt[s_]], writes=[T_pbp[pq]])
                            S.op("dve", lambda e, pq=pq, jc=jc: e.tensor_tensor(
                                out=t2all[:, jc, :], in0=pbp[0][:, 0:TB], in1=gt_[s_][:, 8 + jc, :], op=ALU.mult),
                                 reads=[T_pbp[pq], T_gt[s_]], writes=[T_t2[jc]])

                        pend = None
                        for g in range(16):
                            p = c3["sp"] % 2
                            c3["sp"] += 1
                            fns = [lambda e, p=p, c=c, g=g: e.matmul(
                                spp[p][:, c * 128:(c + 1) * 128], lhsT=vhat[vs][:, c, g * 128:(g + 1) * 128],
                                rhs=wsT_bf[:, g, :], start=True, stop=True) for c in range(2)]
                            S.op("pe", fns, reads=[T_vhat[vs][0], T_vhat[vs][1], T_wsT], writes=[T_spp[p]])
                            S.op("dve", lambda e, p=p, g=g: e.scalar_tensor_tensor(
                                out=tmp[p][:], in0=spp[p][:, 0:TB], scalar=lng[:, g:g + 1],
                                in1=biasT[:, g, :, :].rearrange("p a b -> p (a b)"), op0=ALU.mult, op1=ALU.add),
                                 reads=[T_spp[p], T_lng, T_biasT], writes=[T_tmp[p]])
                            if pend is not None:
                                pool_mult(*pend)
                            pend = (p, g)
                            if g % 2 == 1:
                                pb(g // 2)
                        pool_mult(*pend)

                    def phase_pa(bi):
                        s_ = bi % 2
                        for jc in range(8):
                            p = c3["pa"] % 2
                            c3["pa"] += 1
                            fns = [lambda e, p=p, g=g, jc=jc: e.matmul(
                                pap[p][:, 0:TB], lhsT=wpa[:, g, jc * 128:(jc + 1) * 128], rhs=uzt[s_][:, g, :],
                                start=(g == 0), stop=(g == 15)) for g in range(16)]
                            S.op("pe", fns, reads=T_wpa + T_uzt[s_], writes=[T_pap[p]])
                            S.op("dve", lambda e, p=p, jc=jc: e.tensor_tensor(
                                out=t1[p][:], in0=pap[p][:, 0:TB], in1=gt_[s_][:, jc, :], op=ALU.mult),
                                 reads=[T_pap[p], T_gt[s_]], writes=[T_t1[p]])
                            S.op("pool", lambda e, p=p, jc=jc: e.tensor_tensor(
                                out=mT[:, jc, :], in0=t1[p][:], in1=t2all[:, jc, :], op=ALU.add),
                                 reads=[T_t1[p], T_t2[jc]], writes=[T_mT[jc]])

                    def phase_out(bi):
                        s_ = bi % 2
                        t0 = bi * TB
                        for c in range(2):
                            yi = c3["y"] % 2
                            c3["y"] += 1
                            rbs = [(c3["rp"] + hf) % 3 for hf in range(2)]
                            c3["rp"] += 2
                            for hf in range(2):
                                rb = rbs[hf]
                                fns = [lambda e, rb=rb, hf=hf, k=k, c=c: e.matmul(
                                    rp[rb][:], lhsT=mT[:, k, c * 128:(c + 1) * 128], rhs=wo[:, k, hf * 512:(hf + 1) * 512],
                                    start=(k == 0), stop=(k == 7)) for k in range(8)]
                                S.op("pe", fns, reads=T_mT + T_wo, writes=[T_rp[rb]])
                                S.op("act", lambda e, rb=rb, hf=hf, yi=yi: e.activation(
                                    out=junk3[:], in_=rp[rb][:], func=AF.Square, accum_out=ssr[yi][:, hf:hf + 1]),
                                     reads=[T_rp[rb]], writes=[T_junk3, T_ssr[yi]])
                            S.op("dve", lambda e, yi=yi: e.tensor_tensor(out=ssr[yi][:, 2:3], in0=ssr[yi][:, 0:1], in1=ssr[yi][:, 1:2], op=ALU.add),
                                 writes=[T_ssr[yi]])
                            S.op("act", lambda e, yi=yi: e.activation(out=ssr[yi][:, 3:4], in_=ssr[yi][:, 2:3], func=AF.Sqrt, bias=EPS, scale=1.0 / D),
                                 writes=[T_ssr[yi]])
                            S.op("dve", lambda e, yi=yi: e.reciprocal(out=ssr[yi][:, 2:3], in_=ssr[yi][:, 3:4]), writes=[T_ssr[yi]])
                            for hf in range(2):
                                rb = rbs[hf]
                                S.op("dve", lambda e, rb=rb, hf=hf, yi=yi: e.scalar_tensor_tensor(
                                    out=yt[yi][:, hf * 512:(hf + 1) * 512], in0=rp[rb][:], scalar=ssr[yi][:, 2:3],
                                    in1=gpost_bc[:, hf * 512:(hf + 1) * 512], op0=ALU.mult, op1=ALU.mult),
                                     reads=[T_rp[rb], T_ssr[yi], T_gpost], writes=[T_yt[yi]])
                            S.op("pool", lambda e, yi=yi, c=c: e.tensor_tensor(
                                out=ot[yi][:], in0=yt[yi][:], in1=xrt[s_][:, c, :], op=ALU.add),
                                 reads=[T_yt[yi], T_xrt[s_]], writes=[T_ot[yi]])
                            r0 = t0 + c * 128
                            S.dma("pool", sc["xo"][r0:r0 + 128, :], ot[yi][:], reads=[T_ot[yi]],
                                  writes=[dtile("xo", r0 // 128)])

                    load3a(0)
                    load3b(0)
                    ln_sums(0)
                    ln_small(0)
                    ln_vhat(0)
                    for bi in range(nblk):
                        nxt = bi + 1 < nblk
                        if nxt:
                            load3a(bi + 1)
                            load3b(bi + 1)
                            ln_sums(bi + 1)
                        phase_spatial_pb(bi)
                        if nxt:
                            ln_small(bi + 1)
                            ln_vhat(bi + 1)
                        phase_pa(bi)
                        phase_out(bi)
                S.barrier()
            x_src = sc["xo"]

        S.barrier()
    return nc


def _mask_table():
    p = np.arange(128)[:, None]
    xx = np.arange(MW)[None, :]
    d = (p - xx + 1024).astype(np.int64)
    ad = np.abs(d)
    mult = (ad <= 64).astype(np.float64) + ((ad <= 256) & (d % 4 == 0)) + ((ad <= 1024) & (d % 16 == 0))
    slopes = 2.0 ** (-8.0 * (np.arange(8) + 1.0) / 8.0)
    T = mult[None] * np.exp(-slopes[:, None, None] * ad[None].astype(np.float64))
    T[T < 1e-37] = 0.0
    return np.ascontiguousarray(T.astype(np.float32))


def _head_radius():
    T = _mask_table()
    d = (np.arange(128)[:, None] - np.arange(MW)[None, :] + 1024)
    out = []
    for h in range(8):
        nz = np.abs(d[T[h] != 0.0])
        dmax = int(nz.max())
        out.append(min(8, (dmax + 127) // 128))
    return out


_CONST = {}
HEAD_RADIUS = None


def _consts():
    if not _CONST:
        _CONST["masks"] = _mask_table()
        _CONST["ident"] = np.eye(128, dtype=np.float32)
    return _CONST


def _core_inputs(c, x_local, w_in, b_gate, g_pre, g_post, sgu_ln_g, sgu_ln_b, w_spatial, b_spatial,
                 w_proj_a, w_proj_b, w_out):
    rev = (c % 2 == 1)
    ws = w_spatial[:, :, ::-1, ::-1] if rev else w_spatial
    bs = b_spatial[:, :, ::-1] if rev else b_spatial
    cst = _consts()
    f = np.ascontiguousarray
    return {
        "x": f(x_local),
        "w_in": f(w_in), "w_pa": f(w_proj_a), "w_pb": f(w_proj_b), "w_o": f(w_out),
        "wsT": f(np.transpose(ws, (0, 1, 3, 2))),
        "bs": f(bs.reshape(2, 2048)),
        "g_pre": f(g_pre), "g_post": f(g_post),
        "ln_g_t": f(sgu_ln_g.reshape(2, 16, 128).transpose(0, 2, 1)),
        "ln_b_t": f(sgu_ln_b.reshape(2, 16, 128).transpose(0, 2, 1)),
        "b_gate_t": f(b_gate.reshape(2, 16, 128).transpose(0, 2, 1)),
        "masks": cst["masks"], "ident": cst["ident"],
    }


_NC_CACHE = {}
FUSED = True


def _get_nc(key, *args, **kw):
    global HEAD_RADIUS
    if HEAD_RADIUS is None:
        HEAD_RADIUS = _head_radius()
    if key not in _NC_CACHE:
        _NC_CACHE[key] = _build(*args, **kw)
    return _NC_CACHE[key]


def kernel(x, w_in, b_gate, g_pre, g_post, sgu_ln_g, sgu_ln_b, w_spatial, b_spatial,
           w_proj_a, w_proj_b, w_out):
    arrs = [np.asarray(a, dtype=np.float32) for a in
            (x, w_in, b_gate, g_pre, g_post, sgu_ln_g, sgu_ln_b, w_spatial, b_spatial, w_proj_a, w_proj_b, w_out)]
    x = arrs[0]
    rest = arrs[1:]
    B = x.shape[0]
    xl = []
    for c in range(8):
        b, half = c // 2, c % 2
        xl.append(x[b] if half == 0 else x[b, ::-1])
    if FUSED:
        nc = _get_nc("fused", [(0, 4096, 3072), (1, 3072, 2048)], 4096, 2048)
        in_maps = [_core_inputs(c, xl[c], *rest) for c in range(8)]
        res = run_bass_kernel_spmd(nc, in_maps, core_ids=list(range(8)))
        outs = [r["out"] for r in res.results]
    else:
        nc1 = _get_nc("l0", [(0, 4096, 3072)], 4096, 3072)
        in_maps = [_core_inputs(c, xl[c], *rest) for c in range(8)]
        res = run_bass_kernel_spmd(nc1, in_maps, core_ids=list(range(8)))
        x1 = [r["out"] for r in res.results]
        nc2 = _get_nc("l1", [(1, 3072, 2048)], 3072, 2048)
        in_maps = [_core_inputs(c, x1[c], *rest) for c in range(8)]
        res = run_bass_kernel_spmd(nc2, in_maps, core_ids=list(range(8)))
        outs = [r["out"] for r in res.results]
    out = np.empty((B, SEQ, D), dtype=np.float32)
    for c in range(8):
        b, half = c // 2, c % 2
        if half == 0:
            out[b, 0:2048] = outs[c]
        else:
            out[b, 2048:4096] = outs[c][::-1]
    return out
```

```python
import numpy as np
import concourse.bass as bass
import concourse.mybir as mybir
from concourse.bass_utils import run_bass_kernel_spmd

F32 = mybir.dt.float32
BF16 = mybir.dt.bfloat16
AF = mybir.ActivationFunctionType
ALU = mybir.AluOpType

D = 1024
NIN = 12288
DA = 2048
SEQ = 4096
EPS = 1e-6
MW = 2560
SCALE = 128 ** -0.5
C_U, C_V, C_ZA, C_Q, C_K, C_VB, C_ZB, C_GA, C_GB = 0, 2048, 4096, 6144, 7168, 8192, 9216, 10240, 11264

ENGS = ("pe", "act", "dve", "pool", "sp")
SEM_LIMIT = 30000


class Sig:
    __slots__ = ("sem", "val", "eng")

    def __init__(self, sem, val, eng):
        self.sem, self.val, self.eng = sem, val, eng


class Tile:
    __slots__ = ("w", "rs")

    def __init__(self):
        self.w = None
        self.rs = {}


class Sched:
    def __init__(self, nc, sem_handles):
        self.nc = nc
        self.pool_sems = list(sem_handles)
        self.sems = []
        self.eng = {"pe": nc.tensor, "act": nc.scalar, "dve": nc.vector, "pool": nc.gpsimd, "sp": nc.sync}
        self.seen = {e: {} for e in ENGS}
        self.cur = {e: [self._alloc(), 0] for e in ("pe", "act", "dve", "pool")}
        self.rings = {"sp": [self._alloc() for _ in range(40)], "pool": [self._alloc() for _ in range(24)]}
        self.ring_i = {"sp": 0, "pool": 0}
        self.last = {}

    def _alloc(self):
        h = self.pool_sems.pop()
        self.sems.append(h)
        return len(self.sems) - 1

    def _emit_waits(self, eng, deps):
        best = {}
        for sg in deps:
            if sg is None:
                continue
            if sg.eng == "pe" and eng == "pe":
                continue
            if best.get(sg.sem, 0) < sg.val:
                best[sg.sem] = sg.val
        seen = self.seen[eng]
        for s, v in best.items():
            if seen.get(s, 0) >= v:
                continue
            seen[s] = v
            self.eng[eng].wait_ge(self.sems[s], v)

    def _deps(self, reads, writes):
        deps = []
        for t in reads:
            deps.append(t.w)
        for t in writes:
            deps.append(t.w)
            deps.extend(t.rs.values())
        return deps

    def op(self, eng, fns, reads=(), writes=()):
        if not isinstance(fns, (list, tuple)):
            fns = [fns]
        self._emit_waits(eng, self._deps(reads, writes))
        c = self.cur[eng]
        if c[1] >= SEM_LIMIT:
            c[0] = self._alloc()
            c[1] = 0
        c[1] += 1
        sig = Sig(c[0], c[1], eng)
        h = self.sems[c[0]]
        e = self.eng[eng]
        for f in fns[:-1]:
            f(e)
        fns[-1](e).then_inc(h, 1)
        for t in reads:
            t.rs[eng] = sig
        for t in writes:
            t.w = sig
            t.rs = {}
        self.last[eng] = sig
        return sig

    def dma(self, q, out_ap, in_ap, reads=(), writes=()):
        deps = self._deps(reads, writes)
        ring = self.rings[q]
        i = self.ring_i[q]
        self.ring_i[q] += 1
        s = ring[i % len(ring)]
        k = i // len(ring)
        if k > 0:
            deps.append(Sig(s, 16 * k, "dma"))
        self._emit_waits(q, deps)
        sig = Sig(s, 16 * (k + 1), "dma")
        h = self.sems[s]
        self.eng[q].dma_start(out=out_ap, in_=in_ap).then_inc(h, 16)
        for t in reads:
            t.rs[("d", s)] = sig
        for t in writes:
            t.w = sig
            t.rs = {}
        return sig

    def barrier(self):
        sigs = list(self.last.values())
        for q, ring in self.rings.items():
            n = self.ring_i[q]
            for idx, s in enumerate(ring):
                uses = (n - idx + len(ring) - 1) // len(ring) if n > idx else 0
                if uses > 0:
                    sigs.append(Sig(s, 16 * uses, "dma"))
        for e in ENGS:
            self._emit_waits(e, sigs)


def _build(layer_specs, n_x_rows, n_out_rows, debug=False):
    nc = bass.Bass("TRN2", target_bir_lowering=False)
    dt = nc.dram_tensor
    x_in = dt("x", [n_x_rows, D], F32, kind="ExternalInput").ap()
    w_in = dt("w_in", [2, D, NIN], F32, kind="ExternalInput").ap()
    w_pa = dt("w_pa", [2, DA, D], F32, kind="ExternalInput").ap()
    w_pb = dt("w_pb", [2, D, D], F32, kind="ExternalInput").ap()
    w_o = dt("w_o", [2, D, D], F32, kind="ExternalInput").ap()
    wsT_d = dt("wsT", [2, 16, 128, 128], F32, kind="ExternalInput").ap()
    bs_d = dt("bs", [2, 2048], F32, kind="ExternalInput").ap()
    gpre_d = dt("g_pre", [2, D], F32, kind="ExternalInput").ap()
    gpost_d = dt("g_post", [2, D], F32, kind="ExternalInput").ap()
    lng_d = dt("ln_g_t", [2, 128, 16], F32, kind="ExternalInput").ap()
    lnb_d = dt("ln_b_t", [2, 128, 16], F32, kind="ExternalInput").ap()
    bgt_d = dt("b_gate_t", [2, 128, 16], F32, kind="ExternalInput").ap()
    mask_d = dt("masks", [8, 128, MW], F32, kind="ExternalInput").ap()
    ident_d = dt("ident", [128, 128], F32, kind="ExternalInput").ap()
    out_d = dt("out", [n_out_rows, D], F32, kind="ExternalOutput").ap()

    skind = "ExternalOutput" if debug else "Internal"
    scr = []
    for li, (l, ntW, ntF) in enumerate(layer_specs):
        s = {}
        s["gv"] = dt(f"gv{li}", [ntF, DA], BF16, kind=skind).ap()
        s["uzT"] = dt(f"uzT{li}", [16, 128, ntF], BF16, kind=skind).ap()
        s["qT"] = dt(f"qT{li}", [8, 128, ntF], BF16, kind=skind).ap()
        s["kT"] = dt(f"kT{li}", [8, 128, ntW], BF16, kind=skind).ap()
        s["v1"] = dt(f"v1{li}", [8, 128, ntW // 128, 129], BF16, kind=skind).ap()
        s["zbT"] = dt(f"zbT{li}", [8, 128, ntF], BF16, kind=skind).ap()
        s["gT"] = dt(f"gT{li}", [16, 128, ntF], BF16, kind=skind).ap()
        s["ybT"] = dt(f"ybT{li}", [8, 128, ntF], BF16, kind=skind).ap()
        if li < len(layer_specs) - 1:
            s["xo"] = dt(f"xmid{li}", [ntF, D], F32, kind=skind).ap()
        else:
            s["xo"] = out_d
        scr.append(s)

    from contextlib import ExitStack
    with ExitStack() as top:
        sem_handles = [top.enter_context(nc.semaphore(f"s{i}")) for i in range(84)]
        S = Sched(nc, sem_handles)

        uid = [0]

        def sb(stack, name, shape, dtype):
            uid[0] += 1
            return stack.enter_context(nc.sbuf_tensor(f"sb{uid[0]}_{name}", shape, dtype))

        def ps(stack, name, shape, dtype):
            uid[0] += 1
            return stack.enter_context(nc.psum_tensor(f"ps{uid[0]}_{name}", shape, dtype))

        ident_f = sb(top, "ident_f", [128, 128], F32)
        ident = sb(top, "ident", [128, 128], BF16)
        ones_bf = sb(top, "ones_bf", [128, 128], BF16)
        T_ident_f, T_ident, T_ones = Tile(), Tile(), Tile()
        S.dma("sp", ident_f[:], ident_d[:, :], writes=[T_ident_f])
        S.op("pool", lambda e: e.tensor_copy(out=ident[:], in_=ident_f[:]), reads=[T_ident_f], writes=[T_ident])
        S.op("pool", lambda e: e.memset(ones_bf[:], 1.0), writes=[T_ones])

        x_src = x_in
        for li, (l, ntW, ntF) in enumerate(layer_specs):
            sc = scr[li]
            nTW, nTF = ntW // 128, ntF // 128
            nBF = ntF // 512
            nBW = ntW // 512
            dT = {}

            def dtile(*key):
                if key not in dT:
                    dT[key] = Tile()
                return dT[key]

            with ExitStack() as st:
                hT = sb(st, "hT", [128, 8, ntW], BF16)
                hT_t = [Tile() for _ in range(nTW)]
                gpre_bc = sb(st, "gpre_bc", [128, D], F32)
                T_gpre = Tile()
                S.dma("sp", gpre_bc[:], gpre_d[l].partition_broadcast(128), writes=[T_gpre])
                bgt = sb(st, "bgt", [128, 16], F32)
                T_bgt = Tile()
                S.dma("sp", bgt[:], bgt_d[l], writes=[T_bgt])
                if True:
                    s1 = st
                    wst = [sb(s1, f"wst{i}", [128, 8, 512], F32) for i in range(2)]
                    wb = [sb(s1, f"wb{i}", [128, 8, 512], BF16) for i in range(3)]
                    T_wst = [Tile() for _ in range(2)]
                    T_wb = [Tile() for _ in range(3)]
                    ust = sb(s1, "ust", [128, 4, ntF], BF16)
                    T_ust = {}
                    ost = [sb(s1, f"ost{i}", [128, 512], BF16) for i in range(4)]
                    T_ost = [Tile() for _ in range(4)]
                    zt = [sb(s1, f"zt{i}", [128, 512], BF16) for i in range(2)]
                    T_zt = [Tile() for _ in range(2)]
                    v1st = [sb(s1, f"v1st{i}", [128, 4, 129], BF16) for i in range(2)]
                    T_v1st = [Tile() for _ in range(2)]
                    for i in range(2):
                        S.op("pool", lambda e, i=i: e.memset(v1st[i][:], 1.0), writes=[T_v1st[i]])
                    acc = [ps(s1, f"acc{i}", [128, 512], F32) for i in range(4)]
                    T_acc = [Tile() for _ in range(4)]
                    cnt = {"wst": 0, "wb": 0, "acc": 0, "ost": 0, "zt": 0, "v1": 0}

                    T_wst_h = [[Tile(), Tile()] for _ in range(2)]
                    blk_order = ([C_K, C_K + 512, C_VB, C_VB + 512, C_Q, C_Q + 512, C_ZB, C_ZB + 512,
                                  C_GA, C_GA + 512, C_GB, C_GB + 512] + [C_V + 512 * i for i in range(4)])
                    for i in range(4):
                        blk_order += [C_U + 512 * i, C_ZA + 512 * i]
                    blk_slot = {}

                    def ensure_loaded(k):
                        if k >= len(blk_order) or k in blk_slot:
                            return
                        c0 = blk_order[k]
                        a = cnt["wst"] % 2
                        cnt["wst"] += 1
                        b = cnt["wb"] % 3
                        cnt["wb"] += 1
                        src = w_in[l, :, c0:c0 + 512].rearrange("(kc k) c -> k kc c", k=128)
                        S.dma("sp", wst[a][:, 0:4, :], src[:, 0:4, :], writes=[T_wst_h[a][0]])
                        S.dma("sp", wst[a][:, 4:8, :], src[:, 4:8, :], writes=[T_wst_h[a][1]])
                        S.op("dve", lambda e, a=a, b=b: e.tensor_copy(out=wb[b][:, 0:4, :], in_=wst[a][:, 0:4, :]),
                             reads=[T_wst_h[a][0]], writes=[T_wb[b]])
                        S.op("dve", lambda e, a=a, b=b: e.tensor_copy(out=wb[b][:, 4:8, :], in_=wst[a][:, 4:8, :]),
                             reads=[T_wst_h[a][1]], writes=[T_wb[b]])
                        blk_slot[k] = b

                    blk_next = [0]

                    def load_block2(c0):
                        k = blk_next[0]
                        blk_next[0] += 1
                        assert blk_order[k] == c0, (k, c0)
                        ensure_loaded(k)
                        ensure_loaded(k + 1)
                        return blk_slot[k]

                    def fm_mm(b, c, tb):
                        p = cnt["acc"] % 4
                        cnt["acc"] += 1
                        fns = [lambda e, p=p, b=b, c=c, tb=tb, kc=kc: e.matmul(
                            acc[p][:], lhsT=wb[b][:, kc, c * 128:(c + 1) * 128],
                            rhs=hT[:, kc, tb * 512:(tb + 1) * 512], start=(kc == 0), stop=(kc == 7)) for kc in range(8)]
                        S.op("pe", fns, reads=[T_wb[b]] + hT_t[tb * 4:tb * 4 + 4], writes=[T_acc[p]])
                        return p

                    def tm_mm(b, j):
                        p = cnt["acc"] % 4
                        cnt["acc"] += 1
                        fns = [lambda e, p=p, b=b, j=j, kc=kc: e.matmul(
                            acc[p][:], lhsT=hT[:, kc, j * 128:(j + 1) * 128],
                            rhs=wb[b][:, kc, :], start=(kc == 0), stop=(kc == 7)) for kc in range(8)]
                        S.op("pe", fns, reads=[T_wb[b], hT_t[j]], writes=[T_acc[p]])
                        return p

                    def fm_simple(c0, nblk_tok, func, dst, dkey, chunk0, bias_col0=None):
                        b = load_block2(c0)
                        for c in range(4):
                            for tb in range(nblk_tok):
                                p = fm_mm(b, c, tb)
                                o = cnt["ost"] % 4
                                cnt["ost"] += 1
                                if bias_col0 is None:
                                    S.op("act", lambda e, o=o, p=p: e.activation(out=ost[o][:], in_=acc[p][:], func=func),
                                         reads=[T_acc[p]], writes=[T_ost[o]])
                                else:
                                    bc = bias_col0 + c
                                    S.op("act", lambda e, o=o, p=p, bc=bc: e.activation(
                                        out=ost[o][:], in_=acc[p][:], func=func, bias=bgt[:, bc:bc + 1]),
                                         reads=[T_acc[p], T_bgt], writes=[T_ost[o]])
                                S.dma("pool", dst[chunk0 + c][:, tb * 512:(tb + 1) * 512], ost[o][:],
                                      reads=[T_ost[o]], writes=[dtile(dkey, chunk0 + c, tb)])

                if True:
                    s0 = st
                    NX = 6
                    xt = [sb(s0, f"xt{i}", [128, D], F32) for i in range(NX)]
                    xn = [sb(s0, f"xn{i}", [128, D], BF16) for i in range(4)]
                    junk = sb(s0, "junk0", [128, D], F32)
                    ssb = [sb(s0, f"ss{i}", [128, 4], F32) for i in range(2)]
                    pT = [ps(s0, f"pT{i}", [128, 8, 128], BF16) for i in range(4)]
                    T_xt = [Tile() for _ in range(NX)]
                    T_xn = [Tile() for _ in range(4)]
                    T_junk = Tile()
                    T_ss = [Tile() for _ in range(2)]
                    T_pT = [Tile() for _ in range(4)]
                    ssall = sb(s0, "ssall", [128, nTW], F32)
                    rstdall = sb(s0, "rstdall", [128, nTW], F32)
                    T_ssA = [Tile() for _ in range(nBW)]
                    T_rstdA = [Tile() for _ in range(nBW)]
                    xcnt = [0]

                    def phaseA(tb):
                        for j in range(tb * 4, tb * 4 + 4):
                            a = xcnt[0] % NX
                            xcnt[0] += 1
                            S.dma("sp", xt[a][:], x_src[j * 128:(j + 1) * 128, :], writes=[T_xt[a]])
                            S.op("act", lambda e, a=a, j=j: e.activation(out=junk[:], in_=xt[a][:], func=AF.Square,
                                                                         accum_out=ssall[:, j:j + 1]),
                                 reads=[T_xt[a]], writes=[T_junk, T_ssA[tb]])
                        S.op("act", lambda e, tb=tb: e.activation(out=rstdall[:, tb * 4:tb * 4 + 4], in_=ssall[:, tb * 4:tb * 4 + 4],
                                                                  func=AF.Sqrt, bias=EPS, scale=1.0 / D),
                             reads=[T_ssA[tb]], writes=[T_rstdA[tb]])
                        S.op("dve", lambda e, tb=tb: e.reciprocal(out=rstdall[:, tb * 4:tb * 4 + 4], in_=rstdall[:, tb * 4:tb * 4 + 4]),
                             writes=[T_rstdA[tb]])

                    phaseA(0)
                    ensure_loaded(0)
                    ensure_loaded(1)
                    if nBW > 1:
                        phaseA(1)

                    def stage0_norm(j):
                        a = xcnt[0] % NX
                        xcnt[0] += 1
                        b = j % 4
                        S.dma("sp", xt[a][:], x_src[j * 128:(j + 1) * 128, :], writes=[T_xt[a]])
                        S.op("dve", lambda e, a=a, b=b, j=j: e.scalar_tensor_tensor(
                            out=xn[b][:], in0=xt[a][:], scalar=rstdall[:, j:j + 1], in1=gpre_bc[:],
                            op0=ALU.mult, op1=ALU.mult),
                             reads=[T_xt[a], T_rstdA[j // 4], T_gpre], writes=[T_xn[b]])

                    def stage0_tr(j):
                        b = j % 4
                        fns = [lambda e, b=b, kc=kc: e.transpose(out=pT[b][:, kc, :], in_=xn[b][:, kc * 128:(kc + 1) * 128],
                                                                 identity=ident[:]) for kc in range(8)]
                        S.op("pe", fns, reads=[T_xn[b], T_ident], writes=[T_pT[b]])

                    def stage0_cp(j):
                        b = j % 4
                        S.op("dve", lambda e, b=b, j=j: e.tensor_copy(out=hT[:, :, j * 128:(j + 1) * 128], in_=pT[b][:]),
                             reads=[T_pT[b]], writes=[hT_t[j]])

                    kb = [load_block2(C_K), load_block2(C_K + 512)]
                    for fn_ in (stage0_norm, stage0_tr, stage0_cp):
                        for j in range(0, 4):
                            fn_(j)
                    for tb in range(nBW):
                        if tb + 2 < nBW:
                            phaseA(tb + 2)
                        nxt_j = range(tb * 4 + 4, tb * 4 + 8) if tb + 1 < nBW else ()
                        for j in nxt_j:
                            stage0_norm(j)
                        for bb in range(2):
                            if bb == 1:
                                for j in nxt_j:
                                    stage0_tr(j)
                                for j in nxt_j:
                                    stage0_cp(j)
                            for c in range(4):
                                p = fm_mm(kb[bb], c, tb)
                                o = cnt["ost"] % 4
                                cnt["ost"] += 1
                                S.op("act", lambda e, o=o, p=p: e.activation(out=ost[o][:], in_=acc[p][:], func=AF.Copy),
                                     reads=[T_acc[p]], writes=[T_ost[o]])
                                S.dma("pool", sc["kT"][bb * 4 + c][:, tb * 512:(tb + 1) * 512], ost[o][:],
                                      reads=[T_ost[o]], writes=[dtile("kT", bb * 4 + c, tb)])
                    for bb in range(2):
                        b = load_block2(C_VB + bb * 512)
                        for j in range(nTW):
                            p = tm_mm(b, j)
                            o = cnt["v1"] % 2
                            cnt["v1"] += 1
                            S.op("act", lambda e, o=o, p=p: e.activation(
                                out=v1st[o][:, :, 0:128], in_=acc[p][:].rearrange("p (h e) -> p h e", e=128), func=AF.Copy),
                                 reads=[T_acc[p]], writes=[T_v1st[o]])
                            S.dma("pool", sc["v1"][bb * 4:(bb + 1) * 4, :, j, :].rearrange("h p e -> p h e"), v1st[o][:],
                                  reads=[T_v1st[o]], writes=[dtile("v1", bb, j)])
                    for bb in range(2):
                        fm_simple(C_Q + bb * 512, nBF, AF.Copy, sc["qT"], "qT", bb * 4)
                    for bb in range(2):
                        fm_simple(C_ZB + bb * 512, nBF, AF.Silu, sc["zbT"], "zbT", bb * 4)
                    for bb in range(2):
                        fm_simple(C_GA + bb * 512, nBF, AF.Sigmoid, sc["gT"], "gT", bb * 4, bias_col0=bb * 4)
                    for bb in range(2):
                        fm_simple(C_GB + bb * 512, nBF, AF.Sigmoid, sc["gT"], "gT", 8 + bb * 4, bias_col0=8 + bb * 4)
                    for bb in range(4):
                        b = load_block2(C_V + bb * 512)
                        for j in range(nTF):
                            p = tm_mm(b, j)
                            o = cnt["ost"] % 4
                            cnt["ost"] += 1
                            S.op("act", lambda e, o=o, p=p: e.activation(out=ost[o][:], in_=acc[p][:], func=AF.Gelu_apprx_tanh),
                                 reads=[T_acc[p]], writes=[T_ost[o]])
                            S.dma("pool", sc["gv"][j * 128:(j + 1) * 128, bb * 512:(bb + 1) * 512], ost[o][:],
                                  reads=[T_ost[o]], writes=[dtile("gv", bb, j)])
                    for bb in range(4):
                        b = load_block2(C_U + bb * 512)
                        for c in range(4):
                            for tb in range(nBF):
                                p = fm_mm(b, c, tb)
                                tk = (c, tb)
                                if tk not in T_ust:
                                    T_ust[tk] = Tile()
                                S.op("act", lambda e, c=c, tb=tb, p=p: e.activation(
                                    out=ust[:, c, tb * 512:(tb + 1) * 512], in_=acc[p][:], func=AF.Gelu_apprx_tanh),
                                     reads=[T_acc[p]], writes=[T_ust[tk]])
                        b = load_block2(C_ZA + bb * 512)
                        for c in range(4):
                            for tb in range(nBF):
                                p = fm_mm(b, c, tb)
                                z = cnt["zt"] % 2
                                cnt["zt"] += 1
                                S.op("act", lambda e, z=z, p=p: e.activation(out=zt[z][:], in_=acc[p][:], func=AF.Silu),
                                     reads=[T_acc[p]], writes=[T_zt[z]])
                                o = cnt["ost"] % 4
                                cnt["ost"] += 1
                                S.op("dve", lambda e, o=o, z=z, c=c, tb=tb: e.tensor_tensor(
                                    out=ost[o][:], in0=zt[z][:], in1=ust[:, c, tb * 512:(tb + 1) * 512], op=ALU.mult),
                                     reads=[T_zt[z], T_ust[(c, tb)]], writes=[T_ost[o]])
                                S.dma("pool", sc["uzT"][bb * 4 + c][:, tb * 512:(tb + 1) * 512], ost[o][:],
                                      reads=[T_ost[o]], writes=[dtile("uzT", bb * 4 + c, tb)])
                S.barrier()

            with ExitStack() as st:
                wpa = sb(st, "wpa", [128, 16, D], BF16)
                wpb = sb(st, "wpb", [128, 8, D], BF16)
                wo = sb(st, "wo", [128, 8, D], BF16)
                T_wpa = [Tile() for _ in range(16)]
                T_wpb = [Tile() for _ in range(8)]
                T_wo = [Tile() for _ in range(8)]
                wsT_bf = sb(st, "wsT_bf", [128, 16, 128], BF16)
                T_wsT = Tile()
                biasT = sb(st, "biasT", [128, 16, 2, 128], F32)
                T_biasT = Tile()
                lng = sb(st, "lng", [128, 16], F32)
                lnb = sb(st, "lnb", [128, 16], F32)
                T_lng, T_lnb = Tile(), Tile()
                gpost_bc = sb(st, "gpost_bc", [128, D], F32)
                T_gpost = Tile()
                S.dma("sp", lng[:], lng_d[l], writes=[T_lng])
                S.dma("sp", lnb[:], lnb_d[l], writes=[T_lnb])
                S.dma("sp", gpost_bc[:], gpost_d[l].partition_broadcast(128), writes=[T_gpost])

                with ExitStack() as s2:
                    wst2 = [sb(s2, f"wst2_{i}", [128, 2, D], F32) for i in range(2)]
                    T_wst2 = [Tile() for _ in range(2)]
                    wcnt = [0]

                    def load_proj(src_rows_ap, dst, dst_tiles, kc0, nkc):
                        a = wcnt[0] % 2
                        wcnt[0] += 1
                        S.dma("sp", wst2[a][:, 0:nkc, :], src_rows_ap.rearrange("(kc k) c -> k kc c", k=128),
                              writes=[T_wst2[a]])
                        S.op("pool", lambda e, a=a: e.tensor_copy(out=dst[:, kc0:kc0 + nkc, :], in_=wst2[a][:, 0:nkc, :]),
                             reads=[T_wst2[a]], writes=dst_tiles[kc0:kc0 + nkc])

                    proj_jobs = []
                    for q4 in range(8):
                        proj_jobs.append((w_pa[l, q4 * 256:(q4 + 1) * 256, :], wpa, T_wpa, q4 * 2, 2))
                    for q4 in range(4):
                        proj_jobs.append((w_pb[l, q4 * 256:(q4 + 1) * 256, :], wpb, T_wpb, q4 * 2, 2))
                    for q4 in range(4):
                        proj_jobs.append((w_o[l, q4 * 256:(q4 + 1) * 256, :], wo, T_wo, q4 * 2, 2))

                    wsT_f = wst2[0][:].rearrange("p a (g q) -> p (a g) q", q=128)
                    bs_bc = wst2[1][:].rearrange("p a b -> p (a b)")
                    T_wsTf, T_bsbc = T_wst2[0], T_wst2[1]
                    S.dma("sp", wsT_f, wsT_d[l].rearrange("g q p -> q g p"), writes=[T_wsTf])
                    S.dma("sp", bs_bc, bs_d[l].partition_broadcast(128), writes=[T_bsbc])
                    S.op("pool", lambda e: e.tensor_copy(out=wsT_bf[:], in_=wsT_f), reads=[T_wsTf], writes=[T_wsT])

                    kTh = [sb(s2, f"kTh{i}", [128, ntW], BF16) for i in range(2)]
                    v1h = [sb(s2, f"v1h{i}", [128, nTW, 129], BF16) for i in range(2)]
                    qTh = [sb(s2, f"qTh{i}", [128, ntF], BF16) for i in range(2)]
                    zbh = [sb(s2, f"zbh{i}", [128, ntF], BF16) for i in range(2)]
                    mkf = [sb(s2, "mkf0", [128, MW], F32)]
                    mk = [sb(s2, f"mk{i}", [128, MW], BF16) for i in range(2)]
                    T_kTh = [Tile() for _ in range(2)]
                    T_v1h = [Tile() for _ in range(2)]
                    T_qTh = [Tile() for _ in range(2)]
                    T_zbh = [Tile() for _ in range(2)]
                    T_mkf = [Tile()]
                    T_mk = [Tile() for _ in range(2)]
                    LOOK = 3
                    NS = LOOK + 1
                    NE = LOOK + 2
                    et = [sb(s2, f"et{i}", [128, 512], BF16) for i in range(NE)]
                    pt = [sb(s2, f"pt{i}", [128, 512], BF16) for i in range(NE)]
                    T_et = [Tile() for _ in range(NE)]
                    T_pt = [Tile() for _ in range(NE)]
                    on_t = [sb(s2, f"on{i}", [128, 128], BF16) for i in range(2)]
                    T_on = [Tile() for _ in range(2)]
                    rden = [sb(s2, f"rden{i}", [128, 2], F32) for i in range(2)]
                    T_rden = [Tile() for _ in range(2)]
                    ybst = [sb(s2, f"ybst{i}", [128, 512], BF16) for i in range(2)]
                    T_ybst = [Tile() for _ in range(2)]
                    ybc = [sb(s2, f"ybc{i}", [128, 512], BF16) for i in range(2)]
                    T_ybc = [Tile() for _ in range(2)]
                    stp = [ps(s2, f"stp{i}", [128, 512], F32) for i in range(NS)]
                    ops_ = [ps(s2, f"ops{i}", [128, 512], F32) for i in range(2)]
                    tpp = [ps(s2, f"tpp{i}", [128, 1024], BF16) for i in range(2)]
                    T_stp = [Tile() for _ in range(NS)]
                    T_ops = [Tile() for _ in range(2)]
                    T_tpp = [Tile() for _ in range(2)]

                    for g4 in range(4):
                        pslot = g4 % 2
                        fns = [lambda e, pslot=pslot, g=g4 * 4 + gi, gi=gi: e.matmul(
                            stp[pslot][:, gi * 128:(gi + 1) * 128], lhsT=ones_bf[:], rhs=wsT_bf[:, g, :],
                            start=True, stop=True) for gi in range(4)]
                        S.op("pe", fns, reads=[T_ones, T_wsT], writes=[T_stp[pslot]])
                        for gi in range(4):
                            g = g4 * 4 + gi
                            for cdup in range(2):
                                S.op("dve", lambda e, pslot=pslot, g=g, gi=gi, cdup=cdup: e.scalar_tensor_tensor(
                                    out=biasT[:, g, cdup, :], in0=stp[pslot][:, gi * 128:(gi + 1) * 128], scalar=lnb[:, g:g + 1],
                                    in1=bs_bc[:, g * 128:(g + 1) * 128], op0=ALU.mult, op1=ALU.add),
                                     reads=[T_stp[pslot], T_lnb, T_bsbc], writes=[T_biasT])

                    def load_head(h):
                        s_ = h % 2
                        S.dma("sp", kTh[s_][:], sc["kT"][h], reads=[dtile("kT", h, tb) for tb in range(nBW)], writes=[T_kTh[s_]])
                        S.dma("sp", v1h[s_][:], sc["v1"][h], reads=[dtile("v1", h // 4, j) for j in range(nTW)], writes=[T_v1h[s_]])
                        S.dma("sp", qTh[s_][:], sc["qT"][h], reads=[dtile("qT", h, tb) for tb in range(nBF)], writes=[T_qTh[s_]])
                        S.dma("sp", zbh[s_][:], sc["zbT"][h], reads=[dtile("zbT", h, tb) for tb in range(nBF)], writes=[T_zbh[s_]])
                        S.dma("sp", mkf[0][:], mask_d[h], writes=[T_mkf[0]])
                        S.op("pool", lambda e, s_=s_: e.tensor_copy(out=mk[s_][:], in_=mkf[0][:]), reads=[T_mkf[0]], writes=[T_mk[s_]])

                    load_head(0)
                    tix = [0]
                    for h in range(8):
                        hs = h % 2
                        if h + 1 < 8:
                            load_head(h + 1)
                        for _ in range(2):
                            if proj_jobs:
                                load_proj(*proj_jobs.pop(0))
                        R = HEAD_RADIUS[h]
                        groups = []
                        for i in range(nTF):
                            j_hi = min(nTW - 1, i + R)
                            j_lo = max(0, i - R)
                            js = list(range(j_hi, j_lo - 1, -1))
                            gl = [js[a:a + 4] for a in range(0, len(js), 4)]
                            for gi, g in enumerate(gl):
                                groups.append((i, g, gi == 0, gi == len(gl) - 1, tix[0]))
                            tix[0] += 1
                        G = len(groups)

                        def emit_qk(idx):
                            i, js, first, last, tx = groups[idx]
                            n = len(js)
                            ss_, es = idx % NS, idx % NE
                            fns = [lambda e, ss_=ss_, m=m, j=j, i=i: e.matmul(
                                stp[ss_][:, m * 128:(m + 1) * 128], lhsT=kTh[hs][:, j * 128:(j + 1) * 128],
                                rhs=qTh[hs][:, i * 128:(i + 1) * 128], start=True, stop=True) for m, j in enumerate(js)]
                            S.op("pe", fns, reads=[T_kTh[hs], T_qTh[hs]], writes=[T_stp[ss_]])
                            S.op("act", lambda e, ss_=ss_, es=es, n=n: e.activation(
                                out=et[es][:, 0:n * 128], in_=stp[ss_][:, 0:n * 128], func=AF.Exp, scale=SCALE),
                                 reads=[T_stp[ss_]], writes=[T_et[es]])
                            x0 = 1024 - (js[0] - i) * 128
                            S.op("dve", lambda e, es=es, n=n, x0=x0: e.tensor_tensor(
                                out=pt[es][:, 0:n * 128], in0=et[es][:, 0:n * 128], in1=mk[hs][:, x0:x0 + n * 128], op=ALU.mult),
                                 reads=[T_et[es], T_mk[hs]], writes=[T_pt[es]])

                        def emit_pv(idx):
                            i, js, first, last, tx = groups[idx]
                            es = idx % NE
                            osl = tx % 2
                            n = len(js)
                            fns = [lambda e, osl=osl, es=es, m=m, j=j, first=first, last=last, n=n: e.matmul(
                                ops_[osl][:, 0:129], lhsT=pt[es][:, m * 128:(m + 1) * 128], rhs=v1h[hs][:, j, :],
                                start=(first and m == 0), stop=(last and m == n - 1)) for m, j in enumerate(js)]
                            S.op("pe", fns, reads=[T_pt[es], T_v1h[hs]], writes=[T_ops[osl]])

                        def emit_fin_dve(idx):
                            i, js, first, last, tx = groups[idx]
                            osl = tx % 2
                            S.op("dve", lambda e, osl=osl: e.reciprocal(out=rden[osl][:, 0:1], in_=ops_[osl][:, 128:129]),
                                 reads=[T_ops[osl]], writes=[T_rden[osl]])

                        def emit_fin_act(idx):
                            i, js, first, last, tx = groups[idx]
                            osl = tx % 2
                            S.op("act", lambda e, osl=osl: e.activation(
                                out=on_t[osl][:], in_=ops_[osl][:, 0:128], func=AF.Identity, scale=rden[osl][:, 0:1]),
                                 reads=[T_ops[osl], T_rden[osl]], writes=[T_on[osl]])

                        def emit_fin_pe(idx):
                            i, js, first, last, tx = groups[idx]
                            osl = tx % 2
                            c = i % 4
                            tb = i // 4
                            tsl = tb % 2
                            S.op("pe", lambda e, osl=osl, tsl=tsl, c=c: e.transpose(
                                out=tpp[tsl][:, c * 128:(c + 1) * 128], in_=on_t[osl][:], identity=ident[:]),
                                 reads=[T_on[osl], T_ident], writes=[T_tpp[tsl]])

                        def emit_fin_yb(idx):
                            i, js, first, last, tx = groups[idx]
                            c = i % 4
                            tb = i // 4
                            tsl = tb % 2
                            if c == 3:
                                S.op("dve", lambda e, tsl=tsl: e.tensor_copy(out=ybc[tsl][:], in_=tpp[tsl][:, 0:512]),
                                     reads=[T_tpp[tsl]], writes=[T_ybc[tsl]])
                                S.op("pool", lambda e, tsl=tsl, tb=tb: e.tensor_tensor(
                                    out=ybst[tsl][:], in0=ybc[tsl][:], in1=zbh[hs][:, tb * 512:(tb + 1) * 512],
                                    op=ALU.mult),
                                     reads=[T_ybc[tsl], T_zbh[hs]], writes=[T_ybst[tsl]])
                                S.dma("pool", sc["ybT"][h][:, tb * 512:(tb + 1) * 512], ybst[tsl][:],
                                      reads=[T_ybst[tsl]], writes=[dtile("ybT", h, tb)])

                        for idx in range(G + LOOK + 6):
                            for dk, fn in ((1, emit_fin_dve), (2, emit_fin_act), (3, emit_fin_pe), (5, emit_fin_yb)):
                                k = idx - LOOK - dk
                                if 0 <= k < G and groups[k][3]:
                                    fn(k)
                            if idx < G:
                                emit_qk(idx)
                            k = idx - LOOK
                            if 0 <= k < G:
                                emit_pv(k)
                    while proj_jobs:
                        load_proj(*proj_jobs.pop(0))
                S.barrier()

                with ExitStack() as s3:
                    TB = 256
                    gvt = [sb(s3, "gvt0", [128, 2, DA], BF16)]
                    uzt = [sb(s3, f"uzt{i}", [128, 16, TB], BF16) for i in range(2)]
                    gt_ = [sb(s3, f"gtt{i}", [128, 16, TB], BF16) for i in range(2)]
                    ybt = [sb(s3, f"ybt{i}", [128, 8, TB], BF16) for i in range(2)]
                    xrt = [sb(s3, f"xrt{i}", [128, 2, D], F32) for i in range(2)]
                    T_gvt = [Tile()]
                    T_uzt = [[Tile() for _ in range(16)] for _ in range(2)]
                    T_gt = [Tile() for _ in range(2)]
                    T_ybt = [Tile() for _ in range(2)]
                    T_xrt = [Tile() for _ in range(2)]
                    vhat = [sb(s3, f"vhat{i}", [128, 2, DA], BF16) for i in range(2)]
                    T_vhat = [[Tile(), Tile()] for _ in range(2)]
                    lnst = sb(s3, "lnst", [128, 2, 8], F32)
                    T_lnst = [Tile(), Tile()]
                    junkb = sb(s3, "junkb", [128, DA], BF16)
                    T_junkb = Tile()
                    neghalf = sb(s3, "neghalf", [128, 2], F32)
                    T_neghalf = Tile()
                    S.op("pool", lambda e: e.memset(neghalf[:], -0.5), writes=[T_neghalf])
                    tmp = [sb(s3, f"tmp{i}", [128, TB], F32) for i in range(2)]
                    T_tmp = [Tile() for _ in range(2)]
                    t1 = [sb(s3, f"t1_{i}", [128, TB], F32) for i in range(2)]
                    T_t1 = [Tile() for _ in range(2)]
                    t2all = sb(s3, "t2all", [128, 8, TB], F32)
                    T_t2 = [Tile() for _ in range(8)]
                    mT = sb(s3, "mT", [128, 8, TB], BF16)
                    T_mT = [Tile() for _ in range(8)]
                    junk3 = sb(s3, "junk3", [128, 512], F32)
                    T_junk3 = Tile()
                    ssr = [sb(s3, f"ssr{i}", [128, 4], F32) for i in range(2)]
                    T_ssr = [Tile() for _ in range(2)]
                    yt = [sb(s3, f"yt{i}", [128, D], F32) for i in range(2)]
                    T_yt = [Tile() for _ in range(2)]
                    ot = [sb(s3, f"ot{i}", [128, D], F32) for i in range(2)]
                    T_ot = [Tile() for _ in range(2)]
                    spp = [ps(s3, f"spp{i}", [128, 512], F32) for i in range(2)]
                    pbp = [ps(s3, "pbp0", [128, 512], F32)]
                    pap = [ps(s3, f"pap{i}", [128, 512], F32) for i in range(2)]
                    rp = [ps(s3, f"rp{i}", [128, 512], F32) for i in range(3)]
                    T_spp = [Tile() for _ in range(2)]
                    T_pap = [Tile() for _ in range(2)]
                    T_pbp = [Tile()]
                    T_rp = [Tile() for _ in range(3)]
                    nblk = ntF // TB
                    c3 = {"sp": 0, "pa": 0, "pb": 0, "y": 0, "rp": 0}

                    def load3a(bi):
                        t0 = bi * TB
                        S.dma("sp", gvt[0][:], sc["gv"][t0:t0 + TB, :].rearrange("(c p) f -> p c f", p=128),
                              reads=[dtile("gv", bb, t0 // 128 + cc) for bb in range(4) for cc in range(2)], writes=[T_gvt[0]])

                    def load3b(bi):
                        s_ = bi % 2
                        t0 = bi * TB
                        tb512 = t0 // 512
                        S.dma("sp", uzt[s_][:], sc["uzT"][:, :, t0:t0 + TB].rearrange("g e t -> e g t"),
                              reads=[dtile("uzT", g, tb512) for g in range(16)], writes=T_uzt[s_])
                        S.dma("sp", ybt[s_][:], sc["ybT"][:, :, t0:t0 + TB].rearrange("g e t -> e g t"),
                              reads=[dtile("ybT", g, tb512) for g in range(8)], writes=[T_ybt[s_]])
                        S.dma("sp", gt_[s_][:], sc["gT"][:, :, t0:t0 + TB].rearrange("g e t -> e g t"),
                              reads=[dtile("gT", g, tb512) for g in range(16)], writes=[T_gt[s_]])
                        S.dma("sp", xrt[s_][:], x_src[t0:t0 + TB, :].rearrange("(c p) f -> p c f", p=128), writes=[T_xrt[s_]])

                    def ln_sums(bi):
                        for c in range(2):
                            S.op("act", lambda e, c=c: e.activation(out=junkb[:], in_=gvt[0][:, c, :], func=AF.Copy,
                                                                    accum_out=lnst[:, c, 0:1]),
                                 reads=[T_gvt[0]], writes=[T_junkb, T_lnst[c]])
                            S.op("act", lambda e, c=c: e.activation(out=junkb[:], in_=gvt[0][:, c, :], func=AF.Square,
                                                                    accum_out=lnst[:, c, 1:2]),
                                 reads=[T_gvt[0]], writes=[T_junkb, T_lnst[c]])

                    def ln_small(bi):
                        for c in range(2):
                            ops_l = [
                                lambda e, c=c: e.tensor_scalar(out=lnst[:, c, 2:3], in0=lnst[:, c, 0:1], scalar1=-1.0 / DA, scalar2=None, op0=ALU.mult),
                                lambda e, c=c: e.tensor_tensor(out=lnst[:, c, 3:4], in0=lnst[:, c, 2:3], in1=lnst[:, c, 2:3], op=ALU.mult),
                                lambda e, c=c: e.tensor_scalar(out=lnst[:, c, 4:5], in0=lnst[:, c, 1:2], scalar1=1.0 / DA, scalar2=EPS, op0=ALU.mult, op1=ALU.add),
                                lambda e, c=c: e.tensor_tensor(out=lnst[:, c, 5:6], in0=lnst[:, c, 4:5], in1=lnst[:, c, 3:4], op=ALU.subtract),
                                lambda e, c=c: e.tensor_tensor(out=lnst[:, c, 6:7], in0=lnst[:, c, 5:6], in1=neghalf[:, 0:1], op=ALU.pow),
                                lambda e, c=c: e.tensor_tensor(out=lnst[:, c, 7:8], in0=lnst[:, c, 2:3], in1=lnst[:, c, 6:7], op=ALU.mult),
                            ]
                            for f in ops_l:
                                S.op("pool", f, reads=[T_neghalf], writes=[T_lnst[c]])

                    def ln_vhat(bi):
                        vs = bi % 2
                        for c in range(2):
                            S.op("act", lambda e, c=c, vs=vs: e.activation(
                                out=vhat[vs][:, c, :], in_=gvt[0][:, c, :], func=AF.Identity,
                                scale=lnst[:, c, 6:7], bias=lnst[:, c, 7:8]),
                                 reads=[T_gvt[0], T_lnst[c]], writes=[T_vhat[vs][c]])

                    def phase_spatial_pb(bi):
                        s_, vs = bi % 2, bi % 2

                        def pool_mult(p, g):
                            S.op("pool", lambda e, p=p, g=g: e.tensor_tensor(
                                out=uzt[s_][:, g, :], in0=tmp[p][:], in1=uzt[s_][:, g, :], op=ALU.mult),
                                 reads=[T_tmp[p]], writes=[T_uzt[s_][g]])

                        def pb(jc):
                            pq = 0
                            fns = [lambda e, pq=pq, hh=hh, jc=jc: e.matmul(
                                pbp[0][:, 0:TB], lhsT=wpb[:, hh, jc * 128:(jc + 1) * 128], rhs=ybt[s_][:, hh, :],
                                start=(hh == 0), stop=(hh == 7)) for hh in range(8)]
                            S.op("pe", fns, reads=T_wpb + [T_ybt[s_]], writes=[T_pbp[pq]])
                            S.op("dve", lambda e, pq=pq, jc=jc: e.tensor_tensor(
                                out=t2all[:, jc, :], in0=pbp[0][:, 0:TB], in1=gt_[s_][:, 8 + jc, :], op=ALU.mult),
                                 reads=[T_pbp[pq], T_gt[s_]], writes=[T_t2[jc]])

                        pend = None
                        for g in range(16):
                            p = c3["sp"] % 2
                            c3["sp"] += 1
                            fns = [lambda e, p=p, c=c, g=g: e.matmul(
                                spp[p][:, c * 128:(c + 1) * 128], lhsT=vhat[vs][:, c, g * 128:(g + 1) * 128],
                                rhs=wsT_bf[:, g, :], start=True, stop=True) for c in range(2)]
                            S.op("pe", fns, reads=[T_vhat[vs][0], T_vhat[vs][1], T_wsT], writes=[T_spp[p]])
                            S.op("dve", lambda e, p=p, g=g: e.scalar_tensor_tensor(
                                out=tmp[p][:], in0=spp[p][:, 0:TB], scalar=lng[:, g:g + 1],
                                in1=biasT[:, g, :, :].rearrange("p a b -> p (a b)"), op0=ALU.mult, op1=ALU.add),
                                 reads=[T_spp[p], T_lng, T_biasT], writes=[T_tmp[p]])
                            if pend is not None:
                                pool_mult(*pend)
                            pend = (p, g)
                            if g % 2 == 1:
                                pb(g // 2)
                        pool_mult(*pend)

                    def phase_pa(bi):
                        s_ = bi % 2
                        for jc in range(8):
                            p = c3["pa"] % 2
                            c3["pa"] += 1
                            fns = [lambda e, p=p, g=g, jc=jc: e.matmul(
                                pap[p][:, 0:TB], lhsT=wpa[:, g, jc * 128:(jc + 1) * 128], rhs=uzt[s_][:, g, :],
                                start=(g == 0), stop=(g == 15)) for g in range(16)]
                            S.op("pe", fns, reads=T_wpa + T_uzt[s_], writes=[T_pap[p]])
                            S.op("dve", lambda e, p=p, jc=jc: e.tensor_tensor(
                                out=t1[p][:], in0=pap[p][:, 0:TB], in1=gt_[s_][:, jc, :], op=ALU.mult),
                                 reads=[T_pap[p], T_gt[s_]], writes=[T_t1[p]])
                            S.op("pool", lambda e, p=p, jc=jc: e.tensor_tensor(
                                out=mT[:, jc, :], in0=t1[p][:], in1=t2all[:, jc, :], op=ALU.add),
                                 reads=[T_t1[p], T_t2[jc]], writes=[T_mT[jc]])

                    def phase_out(bi):
                        s_ = bi % 2
                        t0 = bi * TB
                        for c in range(2):
                            yi = c3["y"] % 2
                            c3["y"] += 1
                            rbs = [(c3["rp"] + hf) % 3 for hf in range(2)]
                            c3["rp"] += 2
                            for hf in range(2):
                                rb = rbs[hf]
                                fns = [lambda e, rb=rb, hf=hf, k=k, c=c: e.matmul(
                                    rp[rb][:], lhsT=mT[:, k, c * 128:(c + 1) * 128], rhs=wo[:, k, hf * 512:(hf + 1) * 512],
                                    start=(k == 0), stop=(k == 7)) for k in range(8)]
                                S.op("pe", fns, reads=T_mT + T_wo, writes=[T_rp[rb]])
                                S.op("act", lambda e, rb=rb, hf=hf, yi=yi: e.activation(
                                    out=junk3[:], in_=rp[rb][:], func=AF.Square, accum_out=ssr[yi][:, hf:hf + 1]),
                                     reads=[T_rp[rb]], writes=[T_junk3, T_ssr[yi]])
                            S.op("dve", lambda e, yi=yi: e.tensor_tensor(out=ssr[yi][:, 2:3], in0=ssr[yi][:, 0:1], in1=ssr[yi][:, 1:2], op=ALU.add),
                                 writes=[T_ssr[yi]])
                            S.op("act", lambda e, yi=yi: e.activation(out=ssr[yi][:, 3:4], in_=ssr[yi][:, 2:3], func=AF.Sqrt, bias=EPS, scale=1.0 / D),
                                 writes=[T_ssr[yi]])
                            S.op("dve", lambda e, yi=yi: e.reciprocal(out=ssr[yi][:, 2:3], in_=ssr[yi][:, 3:4]), writes=[T_ssr[yi]])
                            for hf in range(2):
                                rb = rbs[hf]
                                S.op("dve", lambda e, rb=rb, hf=hf, yi=yi: e.scalar_tensor_tensor(
                                    out=yt[yi][:, hf * 512:(hf + 1) * 512], in0=rp[rb][:], scalar=ssr[yi][:, 2:3],
                                    in1=gpost_bc[:, hf * 512:(hf + 1) * 512], op0=ALU.mult, op1=ALU.mult),
                                     reads=[T_rp[rb], T_ssr[yi], T_gpost], writes=[T_yt[yi]])
                            S.op("pool", lambda e, yi=yi, c=c: e.tensor_tensor(
                                out=ot[yi][:], in0=yt[yi][:], in1=xrt[s_][:, c, :], op=ALU.add),
                                 reads=[T_yt[yi], T_xrt[s_]], writes=[T_ot[yi]])
                            r0 = t0 + c * 128
                            S.dma("pool", sc["xo"][r0:r0 + 128, :], ot[yi][:], reads=[T_ot[yi]],
                                  writes=[dtile("xo", r0 // 128)])

                    load3a(0)
                    load3b(0)
                    ln_sums(0)
                    ln_small(0)
                    ln_vhat(0)
                    for bi in range(nblk):
                        nxt = bi + 1 < nblk
                        if nxt:
                            load3a(bi + 1)
                            load3b(bi + 1)
                            ln_sums(bi + 1)
                        phase_spatial_pb(bi)
                        if nxt:
                            ln_small(bi + 1)
                            ln_vhat(bi + 1)
                        phase_pa(bi)
                        phase_out(bi)
                S.barrier()
            x_src = sc["xo"]

        S.barrier()
    return nc


def _mask_table():
    p = np.arange(128)[:, None]
    xx = np.arange(MW)[None, :]
    d = (p - xx + 1024).astype(np.int64)
    ad = np.abs(d)
    mult = (ad <= 64).astype(np.float64) + ((ad <= 256) & (d % 4 == 0)) + ((ad <= 1024) & (d % 16 == 0))
    slopes = 2.0 ** (-8.0 * (np.arange(8) + 1.0) / 8.0)
    T = mult[None] * np.exp(-slopes[:, None, None] * ad[None].astype(np.float64))
    T[T < 1e-37] = 0.0
    return np.ascontiguousarray(T.astype(np.float32))


def _head_radius():
    T = _mask_table()
    d = (np.arange(128)[:, None] - np.arange(MW)[None, :] + 1024)
    out = []
    for h in range(8):
        nz = np.abs(d[T[h] != 0.0])
        dmax = int(nz.max())
        out.append(min(8, (dmax + 127) // 128))
    return out


_CONST = {}
HEAD_RADIUS = None


def _consts():
    if not _CONST:
        _CONST["masks"] = _mask_table()
        _CONST["ident"] = np.eye(128, dtype=np.float32)
    return _CONST


def _core_inputs(c, x_local, w_in, b_gate, g_pre, g_post, sgu_ln_g, sgu_ln_b, w_spatial, b_spatial,
                 w_proj_a, w_proj_b, w_out):
    rev = (c % 2 == 1)
    ws = w_spatial[:, :, ::-1, ::-1] if rev else w_spatial
    bs = b_spatial[:, :, ::-1] if rev else b_spatial
    cst = _consts()
    f = np.ascontiguousarray
    return {
        "x": f(x_local),
        "w_in": f(w_in), "w_pa": f(w_proj_a), "w_pb": f(w_proj_b), "w_o": f(w_out),
        "wsT": f(np.transpose(ws, (0, 1, 3, 2))),
        "bs": f(bs.reshape(2, 2048)),
        "g_pre": f(g_pre), "g_post": f(g_post),
        "ln_g_t": f(sgu_ln_g.reshape(2, 16, 128).transpose(0, 2, 1)),
        "ln_b_t": f(sgu_ln_b.reshape(2, 16, 128).transpose(0, 2, 1)),
        "b_gate_t": f(b_gate.reshape(2, 16, 128).transpose(0, 2, 1)),
        "masks": cst["masks"], "ident": cst["ident"],
    }


_NC_CACHE = {}
FUSED = True


def _get_nc(key, *args, **kw):
    global HEAD_RADIUS
    if HEAD_RADIUS is None:
        HEAD_RADIUS = _head_radius()
    if key not in _NC_CACHE:
        _NC_CACHE[key] = _build(*args, **kw)
    return _NC_CACHE[key]


def kernel(x, w_in, b_gate, g_pre, g_post, sgu_ln_g, sgu_ln_b, w_spatial, b_spatial,
           w_proj_a, w_proj_b, w_out):
    arrs = [np.asarray(a, dtype=np.float32) for a in
            (x, w_in, b_gate, g_pre, g_post, sgu_ln_g, sgu_ln_b, w_spatial, b_spatial, w_proj_a, w_proj_b, w_out)]
    x = arrs[0]
    rest = arrs[1:]
    B = x.shape[0]
    xl = []
    for c in range(8):
        b, half = c // 2, c % 2
        xl.append(x[b] if half == 0 else x[b, ::-1])
    if FUSED:
        nc = _get_nc("fused", [(0, 4096, 3072), (1, 3072, 2048)], 4096, 2048)
        in_maps = [_core_inputs(c, xl[c], *rest) for c in range(8)]
        res = run_bass_kernel_spmd(nc, in_maps, core_ids=list(range(8)))
        outs = [r["out"] for r in res.results]
    else:
        nc1 = _get_nc("l0", [(0, 4096, 3072)], 4096, 3072)
        in_maps = [_core_inputs(c, xl[c], *rest) for c in range(8)]
        res = run_bass_kernel_spmd(nc1, in_maps, core_ids=list(range(8)))
        x1 = [r["out"] for r in res.results]
        nc2 = _get_nc("l1", [(1, 3072, 2048)], 3072, 2048)
        in_maps = [_core_inputs(c, x1[c], *rest) for c in range(8)]
        res = run_bass_kernel_spmd(nc2, in_maps, core_ids=list(range(8)))
        outs = [r["out"] for r in res.results]
    out = np.empty((B, SEQ, D), dtype=np.float32)
    for c in range(8):
        b, half = c // 2, c % 2
        if half == 0:
            out[b, 0:2048] = outs[c]
        else:
            out[b, 2048:4096] = outs[c][::-1]
    return out
```
